# Optimizing a Trainium2 kernel written in Bass

```python
import jax, jax.numpy as jnp
from jax import lax
import numpy as np

D_MODEL = 1024
BATCH = 8
SEQ = 2048
DEPTH = 1

RET_HEADS = 8
RET_QK_DIM = 64
RET_V_DIM = 128
RET_QK_WIDTH = RET_HEADS * RET_QK_DIM
RET_WIDTH = RET_HEADS * RET_V_DIM
CHUNK = 128
ROPE_THETA = 10000.0
CONV_WIDTH = D_MODEL
CONV_KERNEL = 31
MIX_WIDTH = RET_WIDTH + CONV_WIDTH
IN_WIDTH = 2 * RET_QK_WIDTH + 2 * RET_WIDTH + 3 * CONV_WIDTH
LN_EPS = 1e-5
DEEPNORM_ALPHA = (2.0 * DEPTH) ** 0.25
DEEPNORM_BETA = (8.0 * DEPTH) ** -0.25

kernel_name = "hybrid_retention_conformer_parallel"


def _layernorm(x, g, b):
    xf = x.astype(jnp.float32)
    mu = jnp.mean(xf, axis=-1, keepdims=True)
    var = jnp.mean(jnp.square(xf - mu), axis=-1, keepdims=True)
    y = (xf - mu) * lax.rsqrt(var + LN_EPS)
    return (y * g.astype(jnp.float32) + b.astype(jnp.float32)).astype(x.dtype)


def _rotary(t, pos):
    d = t.shape[-1]
    inv_freq = ROPE_THETA ** (-jnp.arange(0, d // 2, dtype=jnp.float32) * 2.0 / d)
    ang = pos[:, None] * inv_freq[None, :]
    cos = jnp.cos(ang)[None, :, None, :].astype(t.dtype)
    sin = jnp.sin(ang)[None, :, None, :].astype(t.dtype)
    t1, t2 = t[..., : d // 2], t[..., d // 2:]
    return jnp.concatenate([t1 * cos - t2 * sin, t1 * sin + t2 * cos], axis=-1)


def _retention(q, k, v):
    b, s, h, dk = q.shape
    dv = v.shape[-1]
    nc = s // CHUNK
    log_g = jnp.log1p(-jnp.exp2(-5.0 - jnp.arange(h, dtype=jnp.float32)))
    idx = jnp.arange(CHUNK, dtype=jnp.float32)
    rel = idx[:, None] - idx[None, :]
    intra_decay = jnp.where(rel[None] >= 0,
                            jnp.exp(log_g[:, None, None] * jnp.maximum(rel, 0.0)[None]), 0.0)
    q_decay = jnp.exp(log_g[:, None] * (idx[None, :] + 1.0))
    k_decay = jnp.exp(log_g[:, None] * (CHUNK - 1.0 - idx[None, :]))
    chunk_decay = jnp.exp(log_g * CHUNK)

    qc = q.astype(jnp.float32).reshape(b, nc, CHUNK, h, dk)
    kc = k.astype(jnp.float32).reshape(b, nc, CHUNK, h, dk) * (dk ** -0.5)
    vc = v.astype(jnp.float32).reshape(b, nc, CHUNK, h, dv)

    scores = jnp.einsum('bnihd,bnjhd->bnhij', qc, kc) * intra_decay[None, None]
    inner = jnp.einsum('bnhij,bnjhe->bnihe', scores, vc)

    kv = jnp.einsum('bnjhd,hj,bnjhe->bnhde', kc, k_decay, vc)

    def step(state, kv_n):
        return chunk_decay[None, :, None, None] * state + kv_n, state

    _, prev = lax.scan(step, jnp.zeros((b, h, dk, dv), jnp.float32), jnp.moveaxis(kv, 1, 0))
    prev = jnp.moveaxis(prev, 0, 1)
    cross = jnp.einsum('bnihd,hi,bnhde->bnihe', qc, q_decay, prev)
    return (inner + cross).reshape(b, s, h, dv).astype(q.dtype)


def _head_groupnorm(y, g):
    yf = y.astype(jnp.float32)
    mu = jnp.mean(yf, axis=-1, keepdims=True)
    var = jnp.mean(jnp.square(yf - mu), axis=-1, keepdims=True)
    yn = (yf - mu) * lax.rsqrt(var + LN_EPS)
    b, s, h, dv = y.shape
    return (yn.reshape(b, s, h * dv) * g.astype(jnp.float32)).astype(y.dtype)


def _layer(x, w_in, ret_norm_g, dw_kernel, dw_bias, conv_ln_g, conv_ln_b,
           w_pw2, b_pw2, w_out, post_ln_g, post_ln_b):
    b, s, _ = x.shape
    z = jnp.einsum('bsd,df->bsf', x, w_in)
    o = np.cumsum([0, RET_QK_WIDTH, RET_QK_WIDTH, RET_WIDTH, RET_WIDTH,
                   CONV_WIDTH, CONV_WIDTH, CONV_WIDTH])
    q, k, v, g_ret, a_conv, glu_conv, g_conv = [z[..., o[i]:o[i + 1]] for i in range(7)]

    pos = jnp.arange(s, dtype=jnp.float32)
    q = _rotary(q.reshape(b, s, RET_HEADS, RET_QK_DIM), pos)
    k = _rotary(k.reshape(b, s, RET_HEADS, RET_QK_DIM), pos)
    v = v.reshape(b, s, RET_HEADS, RET_V_DIM)
    ret = _head_groupnorm(_retention(q, k, v), ret_norm_g) * jax.nn.silu(g_ret)

    u = a_conv * jax.nn.sigmoid(glu_conv)
    u = lax.conv_general_dilated(
        u, dw_kernel[:, None, :].astype(u.dtype), window_strides=(1,),
        padding=[(CONV_KERNEL - 1, 0)], dimension_numbers=('NWC', 'WIO', 'NWC'),
        feature_group_count=CONV_WIDTH) + dw_bias
    u = jax.nn.silu(_layernorm(u, conv_ln_g, conv_ln_b))
    u = jnp.einsum('bsc,ce->bse', u, w_pw2) + b_pw2
    conv = u * jax.nn.silu(g_conv)

    hmix = jnp.einsum('bsm,md->bsd', jnp.concatenate([ret, conv], axis=-1), w_out)
    return _layernorm(DEEPNORM_ALPHA * x + hmix, post_ln_g, post_ln_b)


def setup_inputs(seed: int = 0) -> dict:
    key = jax.random.key(seed)
    ks = jax.random.split(key, 12)
    f32 = jnp.float32
    x = jax.random.normal(ks[0], (BATCH, SEQ, D_MODEL), f32)
    col_scale = np.ones((IN_WIDTH,), np.float32)
    v0 = 2 * RET_QK_WIDTH
    col_scale[v0:v0 + RET_WIDTH] = DEEPNORM_BETA
    c0 = 2 * RET_QK_WIDTH + 2 * RET_WIDTH
    col_scale[c0:c0 + CONV_WIDTH] = DEEPNORM_BETA
    w_in = jax.random.normal(ks[1], (DEPTH, D_MODEL, IN_WIDTH), f32) * (D_MODEL ** -0.5) * jnp.asarray(col_scale)
    ret_norm_g = 1.0 + 0.01 * jax.random.normal(ks[2], (DEPTH, RET_WIDTH), f32)
    dw_kernel = jax.random.normal(ks[3], (DEPTH, CONV_KERNEL, CONV_WIDTH), f32) * (CONV_KERNEL ** -0.5)
    dw_bias = 0.01 * jax.random.normal(ks[4], (DEPTH, CONV_WIDTH), f32)
    conv_ln_g = 1.0 + 0.01 * jax.random.normal(ks[5], (DEPTH, CONV_WIDTH), f32)
    conv_ln_b = 0.01 * jax.random.normal(ks[6], (DEPTH, CONV_WIDTH), f32)
    w_pw2 = jax.random.normal(ks[7], (DEPTH, CONV_WIDTH, CONV_WIDTH), f32) * (CONV_WIDTH ** -0.5) * DEEPNORM_BETA
    b_pw2 = 0.01 * jax.random.normal(ks[8], (DEPTH, CONV_WIDTH), f32)
    w_out = jax.random.normal(ks[9], (DEPTH, MIX_WIDTH, D_MODEL), f32) * (MIX_WIDTH ** -0.5) * DEEPNORM_BETA
    post_ln_g = 1.0 + 0.01 * jax.random.normal(ks[10], (DEPTH, D_MODEL), f32)
    post_ln_b = 0.01 * jax.random.normal(ks[11], (DEPTH, D_MODEL), f32)
    return {"x": x, "w_in": w_in, "ret_norm_g": ret_norm_g, "dw_kernel": dw_kernel,
            "dw_bias": dw_bias, "conv_ln_g": conv_ln_g, "conv_ln_b": conv_ln_b,
            "w_pw2": w_pw2, "b_pw2": b_pw2, "w_out": w_out,
            "post_ln_g": post_ln_g, "post_ln_b": post_ln_b}


def reference(x, w_in, ret_norm_g, dw_kernel, dw_bias, conv_ln_g, conv_ln_b,
              w_pw2, b_pw2, w_out, post_ln_g, post_ln_b):
    for layer in range(DEPTH):
        x = _layer(x, w_in[layer], ret_norm_g[layer], dw_kernel[layer], dw_bias[layer],
                   conv_ln_g[layer], conv_ln_b[layer], w_pw2[layer], b_pw2[layer],
                   w_out[layer], post_ln_g[layer], post_ln_b[layer])
    return x
```

```python
import contextlib
import numpy as np
import concourse.bass as bass
import concourse.mybir as mybir
from concourse.bass_utils import run_bass_kernel_spmd

F32 = mybir.dt.float32
BF16 = mybir.dt.bfloat16
AF = mybir.ActivationFunctionType
ALU = mybir.AluOpType

D = 1024
S = 2048
NCORES = 8
H = 8
DK = 64
DV = 128
CH = 128
KCONV = 31
NDT_C = 1
NPAIR_C = (KCONV - NDT_C) // 2
LN_EPS = 1e-5
ALPHA = 2.0 ** 0.25

ENGS = ("pe", "act", "dve", "pool", "sp")


class Op:
    __slots__ = ("eng", "fn", "deps", "sem", "val", "is_dma")

    def __init__(self, eng, fn, is_dma=False):
        self.eng = eng
        self.fn = fn
        self.deps = []
        self.sem = None
        self.val = None
        self.is_dma = is_dma


class Prog:
    def __init__(self, nc, n_dma_sems=10):
        self.nc = nc
        self.streams = {e: [] for e in ENGS}
        self.cnt = {e: 0 for e in ENGS}
        self.n_dma_sems = n_dma_sems
        self.dma_cnt = {}
        self.dma_rr = {}
        self.dma_last = {}
        self.last_writer = {}
        self.readers = {}
        self.barrier_deps = []
        self.all_dma = []

    def _track(self, o, reads, writes, deps):
        ds = list(deps) + list(self.barrier_deps)
        for k in reads:
            w = self.last_writer.get(k)
            if w is not None:
                ds.append(w)
        for k in writes:
            w = self.last_writer.get(k)
            if w is not None:
                ds.append(w)
            ds.extend(self.readers.get(k, ()))
        for k in reads:
            self.readers.setdefault(k, []).append(o)
        for k in writes:
            self.last_writer[k] = o
            self.readers[k] = []
        o.deps = [d for d in ds if d is not None and d is not o]

    def op(self, eng, fn, reads=(), writes=(), deps=()):
        o = Op(eng, fn)
        self.cnt[eng] += 1
        o.sem = eng
        o.val = self.cnt[eng]
        self._track(o, reads, writes, deps)
        self.streams[eng].append(o)
        return o

    def dma(self, eng, fn, reads=(), writes=(), deps=()):
        if eng not in self.dma_rr:
            self.dma_rr[eng] = 0
            self.dma_cnt[eng] = [0] * self.n_dma_sems
        i = self.dma_rr[eng]
        self.dma_rr[eng] = (i + 1) % self.n_dma_sems
        prev = self.dma_last.get((eng, i))
        o = Op(eng, fn, is_dma=True)
        self.dma_cnt[eng][i] += 16
        o.sem = ("dma", eng, i)
        o.val = self.dma_cnt[eng][i]
        self._track(o, reads, writes, list(deps) + ([prev] if prev is not None else []))
        self.dma_last[(eng, i)] = o
        self.streams[eng].append(o)
        self.all_dma.append(o)
        return o

    def snapshot(self):
        deps = []
        for e in ENGS:
            for o in reversed(self.streams[e]):
                if not o.is_dma:
                    deps.append(o)
                    break
        deps.extend(self.dma_last.values())
        return deps

    def barrier(self, include_dma=True):
        deps = []
        for e in ENGS:
            for o in reversed(self.streams[e]):
                if not o.is_dma:
                    deps.append(o)
                    break
        if include_dma:
            deps.extend(self.dma_last.values())
        self.barrier_deps = deps

    def emit(self, final_waits=()):
        nc = self.nc
        with contextlib.ExitStack() as st:
            sems = {}
            for e in ENGS:
                sems[e] = st.enter_context(nc.semaphore("s_" + e))
            for e in self.dma_rr:
                for i in range(self.n_dma_sems):
                    sems[("dma", e, i)] = st.enter_context(nc.semaphore("d_%s_%d" % (e, i)))
            block = st.enter_context(nc.Block())
            engobj = {"pe": block.tensor, "act": block.scalar, "dve": block.vector,
                      "pool": block.gpsimd, "sp": block.sync}

            def make(ename):
                def body(eng):
                    waited = {}

                    def wait_all(deps):
                        need = {}
                        for d in deps:
                            if need.get(d.sem, 0) < d.val:
                                need[d.sem] = d.val
                        for s, v in need.items():
                            if waited.get(s, 0) < v:
                                eng.wait_ge(sems[s], v)
                                waited[s] = v

                    for o in self.streams[ename]:
                        wait_all(o.deps)
                        ins = o.fn(eng)
                        ins.then_inc(sems[o.sem], 16 if o.is_dma else 1)
                    if ename == "sp":
                        wait_all(final_waits)
                return body

            for e in ENGS:
                if self.streams[e] or e == "sp":
                    engobj[e](make(e))


class SB:
    def __init__(self, nc):
        self.nc = nc
        self.off = (nc._sbuf_addr_for_side("left") + 63) // 64 * 64
        self.limit = nc._sbuf_addr_for_side("right")
        self.n = 0
        self.peak = 0

    def alloc(self, shape, dtype):
        esz = 4 if dtype == F32 else 2
        nbytes = int(np.prod(shape[1:])) * esz
        off = (self.off + 63) // 64 * 64
        self.n += 1
        t = self.nc.alloc_sbuf_tensor_at("sb%d" % self.n, list(shape), dtype, offset=off)
        self.off = off + nbytes
        self.peak = max(self.peak, self.off)
        self.last_off = off
        assert self.off <= self.limit, ("SBUF overflow", self.off, self.limit)
        return t

    def alloc_at(self, off, shape, dtype):
        self.n += 1
        return self.nc.alloc_sbuf_tensor_at("sb%d" % self.n, list(shape), dtype, offset=off)

    def mark(self):
        return self.off

    def release(self, m):
        self.off = m


def _w_in_perm():
    oq, ok, ov, og, oa, ogl, ogc = 0, 512, 1024, 2048, 3072, 4096, 5120
    cols = []

    def swapped(base, h):
        return list(range(base + h * 64 + 32, base + h * 64 + 64)) + list(range(base + h * 64, base + h * 64 + 32))

    for g in range(2):
        heads = range(4 * g, 4 * g + 4)
        for h in heads:
            cols += list(range(oq + h * 64, oq + h * 64 + 64))
        for h in heads:
            cols += swapped(oq, h)
        for h in heads:
            cols += list(range(ok + h * 64, ok + h * 64 + 64))
        for h in heads:
            cols += swapped(ok, h)
        cols += list(range(ov + g * 512, ov + g * 512 + 512))
        cols += list(range(og + g * 512, og + g * 512 + 512))
    for cp in range(4):
        for c in (2 * cp, 2 * cp + 1):
            cols += list(range(oa + c * 128, oa + c * 128 + 128))
            cols += list(range(ogl + c * 128, ogl + c * 128 + 128))
    cols += list(range(ogc, ogc + 1024))
    return np.asarray(cols, dtype=np.int64)


BLK_A = lambda g: 4 * g + 0
BLK_B = lambda g: 4 * g + 1
BLK_V = lambda g: 4 * g + 2
BLK_G = lambda g: 4 * g + 3
BLK_AG = lambda cp: 8 + cp
BLK_GC = lambda i: 12 + i
NBLK = 14


def _const_tables():
    f32 = np.float32
    inv_freq = 10000.0 ** (-(np.arange(32, dtype=np.float64) * 2.0) / 64.0)
    pos = np.arange(S, dtype=np.float64)
    ang = pos[:, None] * inv_freq[None, :]
    cos = np.cos(ang).astype(f32)
    sin = np.sin(ang).astype(f32)
    p = np.arange(128)
    d = p % 64
    fi = d % 32
    cos_t = np.ascontiguousarray(cos[:, fi].T)
    sgn = np.where(d < 32, -1.0, 1.0).astype(f32)
    sin_t = np.ascontiguousarray((sin[:, fi] * sgn[None, :]).T)
    hh = np.arange(H, dtype=np.float64)
    log_g = np.log1p(-np.exp2(-5.0 - hh))
    idx = np.arange(CH, dtype=np.float64)
    scale = DK ** -0.5
    causal = (idx[:, None] <= idx[None, :]).astype(np.float64)
    maskT = np.zeros((CH, H, CH), np.float64)
    for h in range(H):
        maskT[:, h, :] = scale * np.exp(-log_g[h] * (idx[:, None] + 1.0)) * causal
    maskT = maskT.reshape(CH, H * CH).astype(f32)
    zeta = np.zeros((CH, H, DK), np.float64)
    for h in range(H):
        zeta[:, h, :] = (scale * np.exp(log_g[h] * (CH - 1.0 - idx)))[:, None]
    zeta = zeta.reshape(CH, H * DK).astype(f32)
    misc = np.zeros((128, 24), f32)
    misc[:, 16:20] = -0.5
    for h in range(H):
        misc[:, h] = (LN_EPS * np.exp(-2.0 * log_g[h] * (idx + 1.0))).astype(f32)
    for pr in range(4):
        for half in range(2):
            misc[half * 64:(half + 1) * 64, 8 + pr] = f32(np.exp(log_g[2 * pr + half] * CH))
    misc[:, 12] = LN_EPS
    ident = np.eye(128, dtype=f32)
    m = np.arange(128)
    sw = np.where(m % 64 < 32, m + 32, m - 32)
    perm = np.zeros((128, 128), f32)
    perm[sw, m] = 1.0
    return cos_t, sin_t, maskT, zeta, misc, ident, perm


def build_program():
    nc = bass.Bass("TRN2", target_bir_lowering=False)
    NDT = NDT_C
    NPAIR = NPAIR_C
    assert NDT + 2 * NPAIR == KCONV

    def din(name, shape):
        return nc.dram_tensor(name, list(shape), F32, kind="ExternalInput").ap()

    xT_d = din("xT", [D, S])
    x_d = din("x", [S, D])
    win_d = din("w_in_p", [D, NBLK * 512])
    wpw_d = din("w_pw2", [D, D])
    wout_d = din("w_out", [2 * D, D])
    vecfm_d = din("vecfm", [128, 280])
    vecpair_d = din("vecpair", [128, 8 * 2 * NPAIR])
    bcv_d = din("bcv", [128, 3 * D])
    cos_d = din("cos_t", [128, S])
    sin_d = din("sin_t", [128, S])
    maskT_d = din("maskT", [128, H * CH])
    zeta_d = din("zeta", [128, H * DK])
    misc_d = din("misc", [128, 24])
    ident_d = din("ident", [128, 128])
    perm_d = din("perm", [128, 128])
    out_d = nc.dram_tensor("out", [S, D], F32, kind="ExternalOutput").ap()
    dscr = nc.dram_tensor("diag_scr", [8, 128, 2 * NPAIR * 64], BF16, kind="Internal").ap()

    sb = SB(nc)
    p = Prog(nc)

    xT_bf = sb.alloc([128, 8, S], BF16)
    mixTc = sb.alloc([128, 8, S], BF16)
    ident_bf = sb.alloc([128, 128], BF16)
    ones_bf = sb.alloc([128, 128], BF16)
    misc = sb.alloc([128, 24], F32)
    perm_sb = sb.alloc([128, 128], F32)

    pb = [nc.alloc_psum_tensor("pb%d" % i, [128, 512], F32) for i in range(8)]
    ptr = pb[7].bitcast(BF16)

    def load_xT(tb):
        p.dma("pool", lambda e: e.dma_start(
            out=xT_bf[:, :, tb * 512:(tb + 1) * 512],
            in_=xT_d[:, tb * 512:(tb + 1) * 512].rearrange("(kc p) t -> p kc t", p=128)),
            writes=[("xT", tb)])

    p.dma("pool", lambda e: e.dma_start(out=ident_bf[:, :], in_=ident_d[:, :]), writes=["ident"])
    load_xT(0)
    p.dma("sp", lambda e: e.dma_start(out=misc[:, :], in_=misc_d[:, :]), writes=["misc"])
    p.dma("sp", lambda e: e.dma_start(out=perm_sb[:, :], in_=perm_d[:, :]), writes=["perm"])
    p.op("dve", lambda e: e.memset(ones_bf[:, :], 1.0 / 1024.0), writes=["ones"])

    def load_wblock(slot_t, slot_key, blk, deps=()):
        return p.dma("pool", lambda e: e.dma_start(
            out=slot_t[:, :, :],
            in_=win_d[:, blk * 512:(blk + 1) * 512].rearrange("(kc p) j -> p kc j", p=128)),
            writes=[slot_key], deps=deps)

    mC = sb.mark()
    conv = sb.alloc([128, 8, 1024], F32)
    diag = [sb.alloc([128, 2, NPAIR, 64], BF16) for _ in range(3)]
    ubuf = [sb.alloc([128, 1056], BF16) for _ in range(3)]
    stk = [sb.alloc([128, 2, 544], BF16) for _ in range(3)]
    vecpair = sb.alloc([128, 8 * 2 * NPAIR], F32)
    I2 = sb.alloc([128, 64], BF16)
    sig = [sb.alloc([128, 512], F32) for _ in range(2)]
    acc = [sb.alloc([128, 512], F32) for _ in range(2)]
    cb = [sb.alloc([128, 1024], BF16) for _ in range(2)]
    sq = [sb.alloc([128, 1024], BF16) for _ in range(2)]
    mean_sb = sb.alloc([128, 1024], F32)
    rstd_sb = sb.alloc([128, 1024], F32)
    tmp_sb = sb.alloc([128, 1024], F32)
    halo = sb.alloc([128, 8, 32], BF16)
    wtap_bf = sb.alloc([128, 248], BF16)
    c_dead_end = sb.mark()
    vecfm = sb.alloc([128, 280], F32)
    wslot = [sb.alloc([128, 8, 512], BF16) for _ in range(4)]
    hT = sb.alloc([128, 8, 1024], BF16)
    NSG = 3
    sg = [sb.alloc([128, 512], F32) for _ in range(NSG)]
    c_end = sb.mark()

    sb.release(mC)
    cos_sb = sb.alloc([128, S], F32)
    off_cos = sb.last_off
    sin_sb = sb.alloc([128, S], F32)
    rslot = []
    for _ in range(4):
        rslot.append(sb.alloc([128, 8, 512], BF16))
        if len(rslot) == 1:
            off_rs0 = sb.last_off
        if len(rslot) == 3:
            off_rs2 = sb.last_off
    maskT = sb.alloc([128, H * CH], F32)
    zeta = sb.alloc([128, H * DK], F32)
    gbc = sb.alloc([128, D], F32)
    assert sb.mark() <= c_dead_end, (sb.mark(), c_dead_end)
    mixTr = sb.alloc([128, 8, S], BF16)
    off_mixTr = sb.last_off
    qz = sb.alloc([128, 2, 2, S], BF16)
    kp = sb.alloc([128, 2, S], BF16)
    ktok = [sb.alloc([128, 256], BF16) for _ in range(3)]
    fbuf = [sb.alloc([128, 512], F32) for _ in range(4)]
    v_sb = [sb.alloc([128, 512], BF16) for _ in range(2)]
    ST_sb = [sb.alloc([128, 512], BF16) for _ in range(2)]
    yn = [sb.alloc([128, 512], F32) for _ in range(2)]
    ret_bf = [sb.alloc([128, 512], BF16) for _ in range(4)]
    state = sb.alloc([128, 256], F32)
    state_bf = [sb.alloc([128, 256], BF16) for _ in range(2)]
    st6 = [sb.alloc([128, 4, 6], F32) for _ in range(2)]
    mv = [sb.alloc([128, 4, 2], F32) for _ in range(2)]
    sm = [sb.alloc([128, 16], F32) for _ in range(2)]
    q32 = [sb.alloc([128, 512], F32) for _ in range(2)]
    r_end = sb.mark()
    wo = [sb.alloc_at(off_cos, [128, 8, D], BF16), sb.alloc_at(off_rs0, [128, 8, D], BF16)]

    def r_prefetch_ab(g, deps=()):
        load_wblock(rslot[0], ("rs", 0), BLK_A(g), deps)
        load_wblock(rslot[1], ("rs", 1), BLK_B(g), deps)

    def r_prefetch_vg(g, deps=()):
        load_wblock(rslot[2], ("rs", 2), BLK_V(g), deps)
        load_wblock(rslot[3], ("rs", 3), BLK_G(g), deps)


    p.dma("sp", lambda e: e.dma_start(out=vecfm[:, :], in_=vecfm_d[:, :]), writes=["vecfm"])
    VB, VG, VLB, VPB = 248, 256, 264, 272

    p.op("dve", lambda e: e.tensor_scalar(out=vecfm[:, 0:248], in0=vecfm[:, 0:248], scalar1=0.5, scalar2=None, op0=ALU.mult),
         writes=["vecfm"])
    p.dma("sp", lambda e: e.dma_start(out=vecpair[:, :], in_=vecpair_d[:, :]), writes=["vecpair"])
    p.op("dve", lambda e: e.tensor_scalar(out=vecpair[:, :], in0=vecpair[:, :], scalar1=0.5, scalar2=None, op0=ALU.mult),
         writes=["vecpair"])
    p.op("dve", lambda e: e.tensor_tensor(out=I2[:, :], in0=ident_bf[:, 0:64], in1=ident_bf[:, 64:128], op=ALU.add),
         reads=["ident"], writes=["I2"])
    NPT = KCONV - NDT
    cnt = {"item": 0}

    def s1_front(hp, tb, c):
        gb = hp * 2 + tb
        i = cnt["item"]
        cnt["item"] += 1
        ub = ubuf[i % 3]
        di = i % 3
        if gb == 0:
            p.op("pool", lambda e: e.memset(ub[:, 0:30], 0.0), writes=[("uh", i % 3)])
        else:
            p.op("pool", lambda e: e.tensor_copy(out=ub[:, 0:30], in_=halo[:, c, 0:30]),
                 reads=[("halo", c)], writes=[("uh", i % 3)])
        if gb == 0:
            def fd0(e):
                ins = None
                for ab in range(2):
                    for j in range(NPAIR):
                        col = (c * 2 + ab) * NPAIR + j
                        ins = e.tensor_scalar(out=diag[di][:, ab, j, :], in0=I2[:, :],
                                              scalar1=vecpair[:, col:col + 1], scalar2=None, op0=ALU.mult)
                return ins
            p.op("dve", fd0, reads=["I2", "vecpair"], writes=[("diag", di)])
            p.dma("sp", lambda e: e.dma_start(out=dscr[c], in_=diag[di][:, :, :, :].rearrange("p a j m -> p (a j m)")),
                  reads=[("diag", di)], writes=[("dscr", c)])
        else:
            p.dma("sp", lambda e: e.dma_start(out=diag[di][:, :, :, :].rearrange("p a j m -> p (a j m)"), in_=dscr[c]),
                  reads=[("dscr", c)], writes=[("diag", di)])
        ws = wslot[c // 2]
        co = (c % 2) * 256
        par = i % 2
        tsl = slice(gb * 512, gb * 512 + 512)

        abk = 0 if par == 0 else 4
        gbk = 1 if par == 0 else 6

        def fa(e):
            ins = None
            for kc in range(8):
                ins = e.matmul(pb[abk][:, :], lhsT=ws[:, kc, co:co + 128], rhs=xT_bf[:, kc, tsl], start=(kc == 0), stop=(kc == 7))
            return ins

        def fg(e):
            ins = None
            for kc in range(8):
                ins = e.matmul(pb[gbk][:, :], lhsT=ws[:, kc, co + 128:co + 256], rhs=xT_bf[:, kc, tsl], start=(kc == 0), stop=(kc == 7))
            return ins
        p.op("pe", fa, reads=[("ws", c // 2), ("xT", gb)], writes=[("pb", abk)])
        p.op("pe", fg, reads=[("ws", c // 2), ("xT", gb)], writes=[("pb", gbk)])
        p.op("act", lambda e: e.activation(out=sig[par][:, :], in_=pb[gbk][:, :], func=AF.Tanh, scale=0.5),
             writes=[("pb", gbk), ("sig", par)])
        p.op("dve", lambda e: e.scalar_tensor_tensor(out=ub[:, 30:542], in0=sig[par][:, :], scalar=1.0, in1=pb[abk][:, :],
                                                     op0=ALU.add, op1=ALU.mult),
             reads=[("sig", par)], writes=[("pb", abk), ("u", i % 3)])
        if gb < 3:
            p.op("pool", lambda e: e.tensor_copy(out=halo[:, c, 0:30], in_=ub[:, 512:542]),
                 reads=[("u", i % 3)], writes=[("halo", c)])
        sk = stk[i % 3]
        rk_ = [("u", i % 3), ("uh", i % 3)]
        p.op("act", lambda e: e.activation(out=sk[0:64, 0, 0:542], in_=ub[0:64, 0:542], func=AF.Copy), reads=rk_, writes=[("stk", i % 3, 0)])
        p.op("act", lambda e: e.activation(out=sk[64:128, 1, 0:542], in_=ub[64:128, 0:542], func=AF.Copy), reads=rk_, writes=[("stk", i % 3, 2)])
        p.dma("sp", lambda e: e.dma_start(out=sk[64:128, 0, 0:541], in_=ub[0:64, 1:542]), reads=rk_, writes=[("stk", i % 3, 1)])
        p.dma("act", lambda e: e.dma_start(out=sk[0:64, 1, 0:541], in_=ub[64:128, 1:542]), reads=rk_, writes=[("stk", i % 3, 3)])
        return (i, tb, c)

    def s1_conv(info):
        i, tb, c = info
        ub = ubuf[i % 3]
        par2 = i % 2
        cbk = 2 if par2 == 0 else 7
        cps = pb[cbk]
        dg = diag[i % 3]
        ac = acc[par2]
        rk = [("u", i % 3), ("uh", i % 3)]

        sk = stk[i % 3]

        def f(e):
            ins = None
            for j in range(NPAIR):
                k0 = NDT + 2 * j
                e.matmul(cps[0:64, :], lhsT=dg[:, 0, j, :], rhs=sk[:, 0, k0:k0 + 512], start=(j == 0), stop=(j == NPAIR - 1))
                ins = e.matmul(cps[64:128, :], lhsT=dg[:, 1, j, :], rhs=sk[:, 1, k0:k0 + 512], start=(j == 0), stop=(j == NPAIR - 1))
            return ins
        p.op("pe", f, reads=[("stk", i % 3, q) for q in range(4)] + [("diag", i % 3)], writes=[("pb", cbk)])
        for k in range(NDT):
            if k == 0:
                p.op("dve", lambda e, k=k: e.tensor_scalar(
                    out=ac[:, :], in0=ub[:, k:k + 512], scalar1=vecfm[:, k * 8 + c:k * 8 + c + 1], scalar2=None, op0=ALU.mult),
                    reads=rk + ["vecfm"], writes=[("acc", par2)])
            else:
                p.op("dve", lambda e, k=k: e.scalar_tensor_tensor(
                    out=ac[:, :], in0=ub[:, k:k + 512], scalar=vecfm[:, k * 8 + c:k * 8 + c + 1], in1=ac[:, :],
                    op0=ALU.mult, op1=ALU.add),
                    reads=rk + ["vecfm"], writes=[("acc", par2)])
        p.op("dve", lambda e: e.scalar_tensor_tensor(
            out=conv[:, c, tb * 512:(tb + 1) * 512], in0=cps[:, :], scalar=vecfm[:, VB + c:VB + c + 1], in1=ac[:, :],
            op0=ALU.add, op1=ALU.add),
            reads=["vecfm"], writes=[("pb", cbk), ("acc", par2), ("conv", c, tb)])

    def s1_stats(c, tb):
        b = c % 2
        bs = slice(tb * 512, (tb + 1) * 512)
        p.op("act", lambda e: e.activation(out=cb[b][:, 0:512], in_=conv[:, c, bs], func=AF.Copy),
             reads=[("conv", c, tb)], writes=[("cb", b)])
        p.op("act", lambda e: e.activation(out=sq[b][:, 0:512], in_=conv[:, c, bs], func=AF.Square),
             reads=[("conv", c, tb)], writes=[("sq", b)])

        def fs(e):
            e.matmul(pb[3][:, :], lhsT=ones_bf[:, :], rhs=cb[b][:, 0:512], start=(c == 0), stop=(c == 7))
            return e.matmul(pb[5][:, :], lhsT=ones_bf[:, :], rhs=sq[b][:, 0:512], start=(c == 0), stop=(c == 7))
        p.op("pe", fs, reads=[("cb", b), ("sq", b), "ones"], writes=[("pb", 3), ("pb", 5)])

    def s2(tb):
        bs = slice(tb * 512, (tb + 1) * 512)
        p.op("dve", lambda e: e.tensor_copy(out=mean_sb[:, bs], in_=pb[3][:, :]),
             writes=[("pb", 3), ("mean", tb)])
        p.op("dve", lambda e: e.tensor_tensor(out=tmp_sb[:, bs], in0=mean_sb[:, bs], in1=mean_sb[:, bs], op=ALU.mult),
             reads=[("mean", tb)], writes=[("tmp", tb)])
        p.op("dve", lambda e: e.tensor_tensor(out=tmp_sb[:, bs], in0=pb[5][:, :], in1=tmp_sb[:, bs], op=ALU.subtract),
             writes=[("pb", 5), ("tmp", tb)])
        p.op("act", lambda e: e.activation(out=tmp_sb[:, bs], in_=tmp_sb[:, bs], func=AF.Sqrt, bias=misc[:, 12:13], scale=1.0),
             reads=["misc"], writes=[("tmp", tb)])
        p.op("dve", lambda e: e.reciprocal(out=rstd_sb[:, bs], in_=tmp_sb[:, bs]),
             reads=[("tmp", tb)], writes=[("rstd", tb)])

    def s2b_tile(c, tb):
        bs = slice(tb * 512, (tb + 1) * 512)
        p.op("dve", lambda e: e.tensor_tensor(out=conv[:, c, bs], in0=conv[:, c, bs], in1=mean_sb[:, bs], op=ALU.subtract),
             reads=[("mean", tb)], writes=[("conv", c, tb)])
        p.op("dve", lambda e: e.tensor_tensor(out=conv[:, c, bs], in0=conv[:, c, bs], in1=rstd_sb[:, bs], op=ALU.mult),
             reads=[("rstd", tb)], writes=[("conv", c, tb)])
        p.op("act", lambda e: e.activation(out=hT[:, c, bs], in_=conv[:, c, bs], func=AF.Silu,
                                           scale=vecfm[:, VG + c:VG + c + 1], bias=vecfm[:, VLB + c:VLB + c + 1]),
             reads=[("conv", c, tb), "vecfm"], writes=[("hT", c, tb)])

    GSLOT = lambda et: 0 if et < 4 else 2
    PSLOT = lambda et: 1 if et < 4 else 3

    def s3_gate(hp, j):
        tb, et = j // 8, j % 8
        gslot = wslot[GSLOT(et)]
        eo = (et % 4) * 128
        par = j % 2
        tsl = slice((hp * 2 + tb) * 512, (hp * 2 + tb) * 512 + 512)
        sgb = sg[j % NSG]

        def fgc(e):
            ins = None
            for kc in range(8):
                ins = e.matmul(pb[par][:, :], lhsT=gslot[:, kc, eo:eo + 128], rhs=xT_bf[:, kc, tsl], start=(kc == 0), stop=(kc == 7))
            return ins
        p.op("pe", fgc, reads=[("ws", GSLOT(et)), ("xT", hp * 2 + tb)], writes=[("pb", par)])
        p.op("act", lambda e: e.activation(out=sgb[:, :], in_=pb[par][:, :], func=AF.Silu),
             writes=[("pb", par), ("sg", j % NSG)])

    def s3_pw(hp, j):
        tb, et = j // 8, j % 8
        wsl = wslot[PSLOT(et)]
        eo = (et % 4) * 128
        par = j % 2
        tsl = slice((hp * 2 + tb) * 512, (hp * 2 + tb) * 512 + 512)
        pbk = 4 if par == 0 else 6
        pwps = pb[pbk]
        sgb = sg[j % NSG]

        def fpw(e):
            ins = None
            for kc in range(8):
                ins = e.matmul(pwps[:, :], lhsT=wsl[:, kc, eo:eo + 128], rhs=hT[:, kc, tb * 512:(tb + 1) * 512],
                               start=(kc == 0), stop=(kc == 7))
            return ins
        p.op("pe", fpw, reads=[("ws", PSLOT(et))] + [("hT", kc, tb) for kc in range(8)], writes=[("pb", pbk)])
        p.op("dve", lambda e: e.scalar_tensor_tensor(
            out=mixTc[:, et, tsl], in0=pwps[:, :], scalar=vecfm[:, VPB + et:VPB + et + 1], in1=sgb[:, :],
            op0=ALU.add, op1=ALU.mult),
            reads=[("sg", j % NSG), "vecfm"], writes=[("pb", pbk)] + [("mix", 8 + et, tsl.start // 128 + q) for q in range(4)])

    def load_pw(slot_i, blk_i):
        p.dma("pool", lambda e: e.dma_start(
            out=wslot[slot_i][:, :, :],
            in_=wpw_d[:, blk_i * 512:(blk_i + 1) * 512].rearrange("(kc p) j -> p kc j", p=128)),
            writes=[("ws", slot_i)])

    def run_s1_block(hp, tb, extra=None):
        q = []
        for c in range(8):
            q.append(s1_front(hp, tb, c))
            if c >= 2:
                s1_conv(q[c - 2])
            if c >= 4:
                s1_stats(c - 4, tb)
            if extra is not None:
                extra(c)
        s1_conv(q[6])
        s1_conv(q[7])
        s1_stats(4, tb)
        s1_stats(5, tb)

    for hp in range(2):
        if hp == 0:
            load_wblock(wslot[0], ("ws", 0), BLK_AG(0))

            def extra0(c):
                if c == 0:
                    load_wblock(wslot[1], ("ws", 1), BLK_AG(1))
                if c == 1:
                    load_wblock(wslot[2], ("ws", 2), BLK_AG(2))
                if c == 2:
                    load_wblock(wslot[3], ("ws", 3), BLK_AG(3))
                if c == 3:
                    load_xT(1)
                if c == 5:
                    load_xT(2)
                if c == 7:
                    load_xT(3)
        else:
            extra0 = None
            load_wblock(wslot[2], ("ws", 2), BLK_AG(2))
            load_wblock(wslot[3], ("ws", 3), BLK_AG(3))

        run_s1_block(hp, 0, extra0)

        def extra1(c):
            if c == 0:
                s1_stats(6, 0)
                s1_stats(7, 0)
            if c == 1:
                s2(0)
            for t in {2: (0,), 3: (1,), 4: (2, 3), 5: (4,), 6: (5, 6), 7: (7,)}.get(c, ()):
                s2b_tile(t, 0)
            if c == 2:
                load_wblock(wslot[0], ("ws", 0), BLK_GC(0))
            if c == 4:
                load_pw(1, 0)
            if c == 6:
                load_wblock(wslot[2], ("ws", 2), BLK_GC(1))
        run_s1_block(hp, 1, extra1)
        load_pw(3, 1)

        s3_gate(hp, 0)
        s3_gate(hp, 1)
        for j in range(8):
            s3_pw(hp, j)
            s3_gate(hp, j + 2)
            if j == 0:
                s1_stats(6, 1)
                s1_stats(7, 1)
                s2(1)
            for t in {1: (0, 1), 2: (2, 3), 3: (4, 5), 4: (6, 7)}.get(j, ()):
                s2b_tile(t, 1)

        if hp == 1:
            snap = p.snapshot()
            p.dma("sp", lambda e: e.dma_start(out=cos_sb[:, :], in_=cos_d[:, :]), writes=["cos"], deps=snap)
            p.dma("sp", lambda e: e.dma_start(out=sin_sb[:, :], in_=sin_d[:, :]), writes=["sin"], deps=snap)
            r_prefetch_ab(0, snap)
            r_prefetch_vg(0, snap)
            p.dma("sp", lambda e: e.dma_start(out=maskT[:, :], in_=maskT_d[:, :]), writes=["maskT"], deps=snap)
            p.dma("sp", lambda e: e.dma_start(out=zeta[:, :], in_=zeta_d[:, :]), writes=["zeta"], deps=snap)
            p.dma("sp", lambda e: e.dma_start(out=gbc[:, :], in_=bcv_d[:, 0:D]), writes=["gbc"], deps=snap)

        for j in range(8, 16):
            s3_pw(hp, j)
            if j + 2 < 16:
                s3_gate(hp, j + 2)
            if hp == 0 and j == 9:
                load_wblock(wslot[0], ("ws", 0), BLK_AG(0))
            if hp == 0 and j == 11:
                load_wblock(wslot[1], ("ws", 1), BLK_AG(1))

    p.barrier(include_dma=False)

    _save = sb.mark()
    sb.release(off_rs2)
    pgb = sb.alloc([128, 2 * D], F32)
    x_sb = [sb.alloc([128, D], F32) for _ in range(2)]
    r_sb = [sb.alloc([128, D], F32) for _ in range(2)]
    ost6 = [sb.alloc([128, 2, 6], F32) for _ in range(2)]
    omv = [sb.alloc([128, 8], F32) for _ in range(2)]
    assert sb.mark() <= off_mixTr, (sb.mark(), off_mixTr)
    sb.release(_save)
    outs = []
    o_deps = []

    def load_x(t):
        p.dma("sp", lambda e: e.dma_start(out=x_sb[t % 2][:, :], in_=x_d[t * 128:(t + 1) * 128, :]), writes=[("x", t % 2)],
              deps=o_deps)

    def o_setup(deps):
        o_deps.extend(deps)
        p.dma("sp", lambda e: e.dma_start(out=pgb[:, :], in_=bcv_d[:, D:3 * D]), writes=["pgb"], deps=o_deps)
        load_x(0)

    def o_tile(t):
        par = t % 2
        rows = slice(t * 128, (t + 1) * 128)
        if t + 1 < 16:
            load_x(t + 1)
        for half in range(2):
            bk = 2 * par + half
            hps = pb[bk]

            def fo(e, hps=hps, half=half):
                ins = None
                for kc in range(16):
                    src = mixTr if kc < 8 else mixTc
                    ins = e.matmul(hps[:, :], lhsT=src[:, kc % 8, rows], rhs=wo[kc // 8][:, kc % 8, half * 512:(half + 1) * 512],
                                   start=(kc == 0), stop=(kc == 15))
                return ins
            p.op("pe", fo, reads=[("mix", kc, t) for kc in range(16)] + [("wout", q4) for q4 in range(4)],
                 writes=[("pb", bk)])
            p.op("dve", lambda e, hps=hps, half=half: e.scalar_tensor_tensor(
                out=r_sb[par][:, half * 512:(half + 1) * 512], in0=x_sb[par][:, half * 512:(half + 1) * 512], scalar=ALPHA,
                in1=hps[:, :], op0=ALU.mult, op1=ALU.add),
                reads=[("x", par)], writes=[("pb", bk), ("r", par, half)])
            p.op("dve", lambda e, half=half: e.bn_stats(out=ost6[par][:, half, :], in_=r_sb[par][:, half * 512:(half + 1) * 512]),
                 reads=[("r", par, half)], writes=[("ost6", par, half)])
        p.op("dve", lambda e: e.bn_aggr(out=omv[par][:, 0:2], in_=ost6[par][:, :, :].rearrange("p a b -> p (a b)")),
             reads=[("ost6", par, 0), ("ost6", par, 1)], writes=[("omv", par)])
        p.op("act", lambda e: e.activation(out=omv[par][:, 2:3], in_=omv[par][:, 1:2], func=AF.Sqrt, bias=misc[:, 12:13], scale=1.0),
             reads=[("omv", par), "misc"], writes=[("osd", par)])
        p.op("dve", lambda e: e.reciprocal(out=omv[par][:, 3:4], in_=omv[par][:, 2:3]),
             reads=[("osd", par)], writes=[("ors", par)])
        p.op("dve", lambda e: e.scalar_tensor_tensor(out=omv[par][:, 4:5], in0=omv[par][:, 0:1], scalar=-1.0,
                                                     in1=omv[par][:, 3:4], op0=ALU.mult, op1=ALU.mult),
             reads=[("omv", par), ("ors", par)], writes=[("onb", par)])
        p.op("act", lambda e: e.activation(out=r_sb[par][:, :], in_=r_sb[par][:, :], func=AF.Identity,
                                           scale=omv[par][:, 3:4], bias=omv[par][:, 4:5]),
             reads=[("r", par, 0), ("r", par, 1), ("ors", par), ("onb", par)], writes=[("r", par, 0), ("r", par, 1)])
        p.op("dve", lambda e: e.tensor_tensor(out=r_sb[par][:, :], in0=r_sb[par][:, :], in1=pgb[:, 0:D], op=ALU.mult),
             reads=[("r", par, 0), ("r", par, 1), "pgb"], writes=[("r", par, 0), ("r", par, 1)])
        p.op("pool", lambda e: e.tensor_tensor(out=r_sb[par][:, :], in0=r_sb[par][:, :], in1=pgb[:, D:2 * D], op=ALU.add),
             reads=[("r", par, 0), ("r", par, 1), "pgb"], writes=[("r", par, 0), ("r", par, 1)])
        outs.append(p.dma("sp", lambda e: e.dma_start(out=out_d[rows, :], in_=r_sb[par][:, :]),
                          reads=[("r", par, 0), ("r", par, 1)]))

    p.op("pool", lambda e: e.memset(qz[64:128, :, 0, :], 0.0), writes=["qzpad0"])
    p.op("pool", lambda e: e.memset(qz[0:64, :, 1, :], 0.0), writes=["qzpad1"])

    def BK(i):
        return [("pb", i)]

    for g in range(2):

        blocks = [(which, slot_i, j, tb) for which, slot_i in (("q", 0), ("k", 1)) for j in range(2) for tb in range(4)]

        def prep_front(i):
            which, slot_i, j, tb = blocks[i]
            par = i % 2
            ws = rslot[slot_i]
            tsl = slice(tb * 512, (tb + 1) * 512)

            def fq(e):
                ins = None
                for kc in range(8):
                    ins = e.matmul(pb[par][:, :], lhsT=ws[:, kc, j * 128:(j + 1) * 128], rhs=xT_bf[:, kc, tsl],
                                   start=(kc == 0), stop=(kc == 7))
                return ins
            p.op("pe", fq, reads=[("rs", slot_i), ("xT", tb)], writes=BK(par))
            p.op("act", lambda e: e.activation(out=q32[par][:, :], in_=pb[par][:, :], func=AF.Copy),
                 writes=BK(par) + [("q32", par)])

        def prep_back(i):
            which, slot_i, j, tb = blocks[i]
            par = i % 2
            tsl = slice(tb * 512, (tb + 1) * 512)
            ta, tb_ = fbuf[par], fbuf[2 + par]
            p.op("pe", lambda e: e.matmul(pb[2 + par][:, :], lhsT=perm_sb[:, :], rhs=q32[par][:, :], start=True, stop=True),
                 reads=[("q32", par), "perm"], writes=BK(2 + par))
            p.op("dve", lambda e: e.tensor_tensor(out=ta[:, :], in0=q32[par][:, :], in1=cos_sb[:, tsl], op=ALU.mult),
                 reads=["cos", ("q32", par)], writes=[("fb", par)])
            p.op("dve", lambda e: e.tensor_tensor(out=tb_[:, :], in0=pb[2 + par][:, :], in1=sin_sb[:, tsl], op=ALU.mult),
                 reads=["sin"], writes=BK(2 + par) + [("fb", 2 + par)])
            if which == "k":
                p.op("pool", lambda e: e.tensor_tensor(out=kp[:, j, tsl], in0=ta[:, :], in1=tb_[:, :], op=ALU.add),
                     reads=[("fb", par), ("fb", 2 + par)], writes=[("kp", j, tb)])
            else:
                def fqa(e):
                    e.tensor_tensor(out=qz[0:64, j, 0, tsl], in0=ta[0:64, :], in1=tb_[0:64, :], op=ALU.add)
                    return e.tensor_tensor(out=qz[64:128, j, 1, tsl], in0=ta[64:128, :], in1=tb_[64:128, :], op=ALU.add)
                p.op("pool", fqa, reads=[("fb", par), ("fb", 2 + par)], writes=[("qz", j, tb)])

        def prep_step(k):
            if k < len(blocks):
                prep_front(k)
            if k >= 1:
                prep_back(k - 1)

        if g == 0:
            for k in range(len(blocks) + 1):
                prep_step(k)

        p.op("pool", lambda e: e.memset(state[:, :], 0.0), writes=["state"])
        def prefetch_wout(q4):
            p.dma("pool", lambda e: e.dma_start(
                out=wo[q4 // 2][:, (q4 % 2) * 4:(q4 % 2) * 4 + 4, :],
                in_=wout_d[q4 * 512:(q4 + 1) * 512, :].rearrange("(kc p) j -> p kc j", p=128)),
                writes=[("wout", q4)] + (["cos", "sin"] if q4 < 2 else [("rs", 0), ("rs", 1)]))

        def stage_g(n, g=g):
            csl = slice(n * 128, (n + 1) * 128)
            gb = fbuf[n % 4]

            def fg_(e):
                ins = None
                for kc in range(8):
                    ins = e.matmul(pb[1][:, :], lhsT=xT_bf[:, kc, csl], rhs=rslot[3][:, kc, :], start=(kc == 0), stop=(kc == 7))
                return ins
            p.op("pe", fg_, reads=[("rs", 3), ("xT", n // 4)], writes=BK(1))
            p.op("act", lambda e: e.activation(out=gb[:, :], in_=pb[1][:, :], func=AF.Silu),
                 writes=BK(1) + [("fb", n % 4)])
            p.op("pool", lambda e: e.tensor_tensor(out=gb[:, :], in0=gb[:, :], in1=gbc[:, g * 512:(g + 1) * 512], op=ALU.mult),
                 reads=["gbc"], writes=[("fb", n % 4)])

        def stage_v(n, g=g):
            par = n % 2
            csl = slice(n * 128, (n + 1) * 128)

            def fv(e):
                ins = None
                for kc in range(8):
                    ins = e.matmul(pb[0][:, :], lhsT=xT_bf[:, kc, csl], rhs=rslot[2][:, kc, :], start=(kc == 0), stop=(kc == 7))
                return ins
            p.op("pe", fv, reads=[("rs", 2), ("xT", n // 4)], writes=BK(0))
            p.op("act", lambda e: e.activation(out=v_sb[par][:, :], in_=pb[0][:, :], func=AF.Copy),
                 writes=BK(0) + [("v", par)])

        def stage_kt(n, g=g):
            kt = ktok[n % 3]

            kvb = pb[5].bitcast(BF16)

            def ftr(e):
                ins = None
                for j in range(2):
                    ins = e.transpose(kvb[:, j * 128:(j + 1) * 128], kp[:, j, n * 128:(n + 1) * 128], ident_bf[:, :])
                return ins
            p.op("pe", ftr, reads=[("kp", 0, n // 4), ("kp", 1, n // 4), "ident"], writes=BK(5))
            p.op("dve", lambda e: e.tensor_tensor(
                out=kt[:, :], in0=kvb[:, 0:256], in1=zeta[:, g * 256:(g + 1) * 256], op=ALU.mult),
                reads=["zeta"], writes=BK(5) + [("ktok", n % 3)])

        def stage_st(n, g=g):
            par = n % 2
            csl = slice(n * 128, (n + 1) * 128)
            sps = pb[2 + par]

            def fst(e):
                ins = None
                for j in range(2):
                    ins = e.matmul(sps[:, j * 256:(j + 1) * 256].rearrange("p (q i) -> p q i", q=2),
                                   lhsT=kp[:, j, csl], rhs=qz[:, j, :, csl], start=True, stop=True)
                return ins
            p.op("pe", fst, reads=[("kp", 0, n // 4), ("kp", 1, n // 4), ("qz", 0, n // 4), ("qz", 1, n // 4), "qzpad0", "qzpad1"],
                 writes=BK(2 + par))
            p.op("dve", lambda e: e.tensor_tensor(out=ST_sb[par][:, :], in0=sps[:, :], in1=maskT[:, g * 512:(g + 1) * 512], op=ALU.mult),
                 reads=["maskT"], writes=BK(2 + par) + [("ST", par)])

        def stage_y1(n, g=g):
            par = n % 2
            csl = slice(n * 128, (n + 1) * 128)
            yps = pb[4] if par == 0 else pb[6]
            ybk = 4 if par == 0 else 6

            def fy(e):
                ins = None
                for hh in range(4):
                    j, q = hh // 2, hh % 2
                    yo = yps[:, hh * 128:(hh + 1) * 128]
                    ins = e.matmul(yo, lhsT=ST_sb[par][:, hh * 128:(hh + 1) * 128],
                                   rhs=v_sb[par][:, hh * 128:(hh + 1) * 128], start=True, stop=(n == 0))
                    if n > 0:
                        ins = e.matmul(yo, lhsT=qz[:, j, q, csl], rhs=state_bf[par][:, j * 128:(j + 1) * 128],
                                       start=False, stop=True)
                return ins
            rd = [("ST", par), ("v", par), ("qz", 0, n // 4), ("qz", 1, n // 4), "qzpad0", "qzpad1"]
            if n > 0:
                rd.append(("sbf", par))
            p.op("pe", fy, reads=rd, writes=BK(ybk))

            if n < 15:
                def fkv(e):
                    ins = None
                    for j in range(2):
                        ins = e.matmul(pb[5][:, j * 256:(j + 1) * 256], lhsT=ktok[n % 3][:, j * 128:(j + 1) * 128],
                                       rhs=v_sb[par][:, j * 256:(j + 1) * 256], start=True, stop=True)
                    return ins
                p.op("pe", fkv, reads=[("ktok", n % 3), ("v", par)], writes=BK(5))

                def fsu(e):
                    ins = None
                    for j in range(2):
                        for hl in range(2):
                            r0 = hl * 64
                            ins = e.scalar_tensor_tensor(
                                out=state[r0:r0 + 64, j * 128:(j + 1) * 128], in0=state[r0:r0 + 64, j * 128:(j + 1) * 128],
                                scalar=misc[r0:r0 + 64, 8 + 2 * g + j: 9 + 2 * g + j],
                                in1=pb[5][r0:r0 + 64, j * 256 + hl * 128: j * 256 + hl * 128 + 128],
                                op0=ALU.mult, op1=ALU.add)
                    return ins
                p.op("dve", fsu, reads=["misc"], writes=BK(5) + ["state"])
                p.op("dve", lambda e: e.tensor_copy(out=state_bf[1 - par][:, :], in_=state[:, :]),
                     reads=["state"], writes=[("sbf", 1 - par)])

            def fbs(e):
                ins = None
                for hh in range(4):
                    ins = e.bn_stats(out=st6[par][:, hh, :], in_=yps[:, hh * 128:(hh + 1) * 128])
                return ins
            p.op("dve", fbs, writes=BK(ybk) + [("st6", par)])

            def fba(e):
                ins = None
                for hh in range(4):
                    ins = e.bn_aggr(out=mv[par][:, hh, :], in_=st6[par][:, hh, :])
                return ins
            p.op("dve", fba, reads=[("st6", par)], writes=[("mv", par)])
            p.op("dve", lambda e: e.tensor_tensor(out=sm[par][:, 0:4], in0=mv[par][:, :, 1], in1=misc[:, g * 4:(g + 1) * 4], op=ALU.add),
                 reads=[("mv", par), "misc"], writes=[("sm0", par)])
            p.op("dve", lambda e: e.tensor_scalar(out=sm[par][:, 12:16], in0=mv[par][:, :, 0], scalar1=-1.0, scalar2=None, op0=ALU.mult),
                 reads=[("mv", par)], writes=[("sm3", par)])

        def stage_y2(n, g=g):
            par = n % 2
            yps = pb[4] if par == 0 else pb[6]
            ybk = 4 if par == 0 else 6
            gb = fbuf[n % 4]
            rb = ret_bf[n % 4]
            p.op("pool", lambda e: e.tensor_tensor(out=sm[par][:, 4:8], in0=sm[par][:, 0:4], in1=misc[:, 16:20], op=ALU.pow),
                 reads=[("sm0", par), "misc"], writes=[("sm1", par)])
            p.op("pool", lambda e: e.tensor_tensor(out=sm[par][:, 8:12], in0=sm[par][:, 12:16], in1=sm[par][:, 4:8], op=ALU.mult),
                 reads=[("sm3", par), ("sm1", par)], writes=[("sm2", par)])

            def fn_(e):
                ins = None
                for hh in range(4):
                    ins = e.activation(out=yn[par][:, hh * 128:(hh + 1) * 128], in_=yps[:, hh * 128:(hh + 1) * 128],
                                       func=AF.Identity, scale=sm[par][:, 4 + hh:5 + hh], bias=sm[par][:, 8 + hh:9 + hh])
                return ins
            p.op("act", fn_, reads=[("sm1", par), ("sm2", par)], writes=BK(ybk) + [("yn", par)])
            p.op("pool", lambda e: e.tensor_tensor(out=rb[:, :], in0=yn[par][:, :], in1=gb[:, :], op=ALU.mult),
                 reads=[("yn", par), ("fb", n % 4)], writes=[("ret", n % 4)])

        def stage_trkt(n_tr, n_kt, g=g):
            def ft(e):
                ins = None
                if n_tr is not None:
                    rb = ret_bf[n_tr % 4]
                    for hh in range(4):
                        ins = e.transpose(ptr[:, hh * 128:(hh + 1) * 128], rb[:, hh * 128:(hh + 1) * 128], ident_bf[:, :])
                if n_kt is not None:
                    for j in range(2):
                        ins = e.transpose(ptr[:, 512 + j * 128:512 + (j + 1) * 128], kp[:, j, n_kt * 128:(n_kt + 1) * 128], ident_bf[:, :])
                return ins
            rd = ["ident"]
            if n_tr is not None:
                rd.append(("ret", n_tr % 4))
            if n_kt is not None:
                rd += [("kp", 0, n_kt // 4), ("kp", 1, n_kt // 4)]
            p.op("pe", ft, reads=rd, writes=["ptrb"])
            if n_kt is not None:
                kt = ktok[n_kt % 3]
                p.op("dve", lambda e: e.tensor_tensor(
                    out=kt[:, :], in0=ptr[:, 512:768], in1=zeta[:, g * 256:(g + 1) * 256], op=ALU.mult),
                    reads=["zeta"], writes=["ptrb", ("ktok", n_kt % 3)])
            if n_tr is not None:
                p.op("act", lambda e: e.activation(
                    out=mixTr[:, 4 * g:4 * g + 4, n_tr * 128:(n_tr + 1) * 128],
                    in_=ptr[:, 0:512].rearrange("p (h t) -> p h t", h=4), func=AF.Copy),
                    writes=["ptrb"] + [("mix", 4 * g + hh, n_tr) for hh in range(4)])

        stage_trkt(None, 0)
        for i in range(16 + 3):
            if g == 1 and i in (2, 5, 8, 11):
                prefetch_wout((i - 2) // 3)
            if g == 0 and i == 2:
                load_wblock(rslot[0], ("rs", 0), BLK_A(1))
            if g == 0 and i == 5:
                load_wblock(rslot[1], ("rs", 1), BLK_B(1))
            if g == 0 and i == 16:
                load_wblock(rslot[2], ("rs", 2), BLK_V(1))
            if g == 0 and i == 17:
                load_wblock(rslot[3], ("rs", 3), BLK_G(1))
            if i < 16:
                stage_g(i)
                stage_v(i)
                stage_st(i)
            if 0 <= i - 1 < 16:
                stage_y1(i - 1)
            if 0 <= i - 3 < 16 or i + 1 < 16:
                stage_trkt(i - 3 if 0 <= i - 3 < 16 else None, i + 1 if i + 1 < 16 else None)
            if 0 <= i - 2 < 16:
                stage_y2(i - 2)
            if g == 1 and i >= 16:
                if i == 16:
                    o_setup(p.snapshot())
                o_tile(2 * (i - 16))
                o_tile(2 * (i - 16) + 1)
            if g == 0 and i >= 16:
                lo, hi = {16: (0, 2), 17: (2, 9), 18: (9, 17)}[i]
                for k in range(lo, hi):
                    prep_step(k)

    for t in range(6, 16):
        o_tile(t)

    p.emit(final_waits=outs)
    return nc


_PROGRAM = None
_CONSTS = None


def kernel(x, w_in, ret_norm_g, dw_kernel, dw_bias, conv_ln_g, conv_ln_b, w_pw2, b_pw2, w_out,
           post_ln_g, post_ln_b):
    global _PROGRAM, _CONSTS
    f32 = np.float32
    x = np.asarray(x, f32)
    w_in = np.asarray(w_in, f32)[0]
    if _CONSTS is None:
        _CONSTS = _const_tables()
    cos_t, sin_t, maskT, zeta, misc, ident, perm = _CONSTS

    w_in_p = np.ascontiguousarray(w_in[:, _w_in_perm()])
    w_pw2_ = np.ascontiguousarray(np.asarray(w_pw2, f32)[0])
    w_out_ = np.ascontiguousarray(np.asarray(w_out, f32)[0])
    vecfm = np.zeros((128, 280), f32)
    dwk = np.asarray(dw_kernel, f32)[0]
    vecfm[:, 0:248] = dwk.reshape(KCONV, 8, 128).transpose(2, 0, 1).reshape(128, 248)

    vecpair = np.zeros((128, 8 * 2 * NPAIR_C), f32)
    pidx = np.arange(128)
    for c in range(8):
        for j in range(NPAIR_C):
            k0 = NDT_C + 2 * j
            vecpair[:, (c * 2 + 0) * NPAIR_C + j] = dwk[k0 + (pidx >= 64), c * 128 + (pidx % 64)]
            vecpair[:, (c * 2 + 1) * NPAIR_C + j] = dwk[k0 + (pidx < 64), c * 128 + 64 + (pidx % 64)]

    def fm(v):
        return np.asarray(v, f32)[0].reshape(8, 128).T

    vecfm[:, 248:256] = fm(dw_bias)
    vecfm[:, 256:264] = fm(conv_ln_g)
    vecfm[:, 264:272] = fm(conv_ln_b)
    vecfm[:, 272:280] = fm(b_pw2)
    bcv = np.empty((128, 3 * D), f32)
    bcv[:, 0:D] = np.asarray(ret_norm_g, f32)[0][None, :]
    bcv[:, D:2 * D] = np.asarray(post_ln_g, f32)[0][None, :]
    bcv[:, 2 * D:3 * D] = np.asarray(post_ln_b, f32)[0][None, :]

    if _PROGRAM is None:
        _PROGRAM = build_program()
    nc = _PROGRAM

    in_maps = []
    for b in range(NCORES):
        in_maps.append({
            "xT": np.ascontiguousarray(x[b].T), "x": np.ascontiguousarray(x[b]),
            "w_in_p": w_in_p, "w_pw2": w_pw2_, "w_out": w_out_,
            "vecfm": vecfm, "vecpair": vecpair, "bcv": bcv, "cos_t": cos_t, "sin_t": sin_t,
            "maskT": maskT, "zeta": zeta, "misc": misc, "ident": ident, "perm": perm,
        })
    res = run_bass_kernel_spmd(nc, in_maps, core_ids=list(range(NCORES)))
    return np.stack([np.asarray(r["out"], f32) for r in res.results], axis=0)
```

```python
import contextlib
import numpy as np
import concourse.bass as bass
import concourse.mybir as mybir
from concourse.bass_utils import run_bass_kernel_spmd

F32 = mybir.dt.float32
BF16 = mybir.dt.bfloat16
AF = mybir.ActivationFunctionType
ALU = mybir.AluOpType

D = 1024
S = 2048
NCORES = 8
H = 8
DK = 64
DV = 128
CH = 128
KCONV = 31
NDT_C = 1
NPAIR_C = (KCONV - NDT_C) // 2
LN_EPS = 1e-5
ALPHA = 2.0 ** 0.25

ENGS = ("pe", "act", "dve", "pool", "sp")


class Op:
    __slots__ = ("eng", "fn", "deps", "sem", "val", "is_dma")

    def __init__(self, eng, fn, is_dma=False):
        self.eng = eng
        self.fn = fn
        self.deps = []
        self.sem = None
        self.val = None
        self.is_dma = is_dma


class Prog:
    def __init__(self, nc, n_dma_sems=10):
        self.nc = nc
        self.streams = {e: [] for e in ENGS}
        self.cnt = {e: 0 for e in ENGS}
        self.n_dma_sems = n_dma_sems
        self.dma_cnt = {}
        self.dma_rr = {}
        self.dma_last = {}
        self.last_writer = {}
        self.readers = {}
        self.barrier_deps = []
        self.all_dma = []

    def _track(self, o, reads, writes, deps):
        ds = list(deps) + list(self.barrier_deps)
        for k in reads:
            w = self.last_writer.get(k)
            if w is not None:
                ds.append(w)
        for k in writes:
            w = self.last_writer.get(k)
            if w is not None:
                ds.append(w)
            ds.extend(self.readers.get(k, ()))
        for k in reads:
            self.readers.setdefault(k, []).append(o)
        for k in writes:
            self.last_writer[k] = o
            self.readers[k] = []
        o.deps = [d for d in ds if d is not None and d is not o]

    def op(self, eng, fn, reads=(), writes=(), deps=()):
        o = Op(eng, fn)
        self.cnt[eng] += 1
        o.sem = eng
        o.val = self.cnt[eng]
        self._track(o, reads, writes, deps)
        self.streams[eng].append(o)
        return o

    def dma(self, eng, fn, reads=(), writes=(), deps=()):
        if eng not in self.dma_rr:
            self.dma_rr[eng] = 0
            self.dma_cnt[eng] = [0] * self.n_dma_sems
        i = self.dma_rr[eng]
        self.dma_rr[eng] = (i + 1) % self.n_dma_sems
        prev = self.dma_last.get((eng, i))
        o = Op(eng, fn, is_dma=True)
        self.dma_cnt[eng][i] += 16
        o.sem = ("dma", eng, i)
        o.val = self.dma_cnt[eng][i]
        self._track(o, reads, writes, list(deps) + ([prev] if prev is not None else []))
        self.dma_last[(eng, i)] = o
        self.streams[eng].append(o)
        self.all_dma.append(o)
        return o

    def snapshot(self):
        deps = []
        for e in ENGS:
            for o in reversed(self.streams[e]):
                if not o.is_dma:
                    deps.append(o)
                    break
        deps.extend(self.dma_last.values())
        return deps

    def barrier(self, include_dma=True):
        deps = []
        for e in ENGS:
            for o in reversed(self.streams[e]):
                if not o.is_dma:
                    deps.append(o)
                    break
        if include_dma:
            deps.extend(self.dma_last.values())
        self.barrier_deps = deps

    def emit(self, final_waits=()):
        nc = self.nc
        with contextlib.ExitStack() as st:
            sems = {}
            for e in ENGS:
                sems[e] = st.enter_context(nc.semaphore("s_" + e))
            for e in self.dma_rr:
                for i in range(self.n_dma_sems):
                    sems[("dma", e, i)] = st.enter_context(nc.semaphore("d_%s_%d" % (e, i)))
            block = st.enter_context(nc.Block())
            engobj = {"pe": block.tensor, "act": block.scalar, "dve": block.vector,
                      "pool": block.gpsimd, "sp": block.sync}

            def make(ename):
                def body(eng):
                    waited = {}

                    def wait_all(deps):
                        need = {}
                        for d in deps:
                            if need.get(d.sem, 0) < d.val:
                                need[d.sem] = d.val
                        for s, v in need.items():
                            if waited.get(s, 0) < v:
                                eng.wait_ge(sems[s], v)
                                waited[s] = v

                    for o in self.streams[ename]:
                        wait_all(o.deps)
                        ins = o.fn(eng)
                        ins.then_inc(sems[o.sem], 16 if o.is_dma else 1)
                    if ename == "sp":
                        wait_all(final_waits)
                return body

            for e in ENGS:
                if self.streams[e] or e == "sp":
                    engobj[e](make(e))


class SB:
    def __init__(self, nc):
        self.nc = nc
        self.off = (nc._sbuf_addr_for_side("left") + 63) // 64 * 64
        self.limit = nc._sbuf_addr_for_side("right")
        self.n = 0
        self.peak = 0

    def alloc(self, shape, dtype):
        esz = 4 if dtype == F32 else 2
        nbytes = int(np.prod(shape[1:])) * esz
        off = (self.off + 63) // 64 * 64
        self.n += 1
        t = self.nc.alloc_sbuf_tensor_at("sb%d" % self.n, list(shape), dtype, offset=off)
        self.off = off + nbytes
        self.peak = max(self.peak, self.off)
        self.last_off = off
        assert self.off <= self.limit, ("SBUF overflow", self.off, self.limit)
        return t

    def alloc_at(self, off, shape, dtype):
        self.n += 1
        return self.nc.alloc_sbuf_tensor_at("sb%d" % self.n, list(shape), dtype, offset=off)

    def mark(self):
        return self.off

    def release(self, m):
        self.off = m


def _w_in_perm():
    oq, ok, ov, og, oa, ogl, ogc = 0, 512, 1024, 2048, 3072, 4096, 5120
    cols = []

    def swapped(base, h):
        return list(range(base + h * 64 + 32, base + h * 64 + 64)) + list(range(base + h * 64, base + h * 64 + 32))

    for g in range(2):
        heads = range(4 * g, 4 * g + 4)
        for h in heads:
            cols += list(range(oq + h * 64, oq + h * 64 + 64))
        for h in heads:
            cols += swapped(oq, h)
        for h in heads:
            cols += list(range(ok + h * 64, ok + h * 64 + 64))
        for h in heads:
            cols += swapped(ok, h)
        cols += list(range(ov + g * 512, ov + g * 512 + 512))
        cols += list(range(og + g * 512, og + g * 512 + 512))
    for cp in range(4):
        for c in (2 * cp, 2 * cp + 1):
            cols += list(range(oa + c * 128, oa + c * 128 + 128))
            cols += list(range(ogl + c * 128, ogl + c * 128 + 128))
    cols += list(range(ogc, ogc + 1024))
    return np.asarray(cols, dtype=np.int64)


BLK_A = lambda g: 4 * g + 0
BLK_B = lambda g: 4 * g + 1
BLK_V = lambda g: 4 * g + 2
BLK_G = lambda g: 4 * g + 3
BLK_AG = lambda cp: 8 + cp
BLK_GC = lambda i: 12 + i
NBLK = 14


def _const_tables():
    f32 = np.float32
    inv_freq = 10000.0 ** (-(np.arange(32, dtype=np.float64) * 2.0) / 64.0)
    pos = np.arange(S, dtype=np.float64)
    ang = pos[:, None] * inv_freq[None, :]
    cos = np.cos(ang).astype(f32)
    sin = np.sin(ang).astype(f32)
    p = np.arange(128)
    d = p % 64
    fi = d % 32
    cos_t = np.ascontiguousarray(cos[:, fi].T)
    sgn = np.where(d < 32, -1.0, 1.0).astype(f32)
    sin_t = np.ascontiguousarray((sin[:, fi] * sgn[None, :]).T)
    hh = np.arange(H, dtype=np.float64)
    log_g = np.log1p(-np.exp2(-5.0 - hh))
    idx = np.arange(CH, dtype=np.float64)
    scale = DK ** -0.5
    causal = (idx[:, None] <= idx[None, :]).astype(np.float64)
    maskT = np.zeros((CH, H, CH), np.float64)
    for h in range(H):
        maskT[:, h, :] = scale * np.exp(-log_g[h] * (idx[:, None] + 1.0)) * causal
    maskT = maskT.reshape(CH, H * CH).astype(f32)
    zeta = np.zeros((CH, H, DK), np.float64)
    for h in range(H):
        zeta[:, h, :] = (scale * np.exp(log_g[h] * (CH - 1.0 - idx)))[:, None]
    zeta = zeta.reshape(CH, H * DK).astype(f32)
    misc = np.zeros((128, 24), f32)
    misc[:, 16:20] = -0.5
    for h in range(H):
        misc[:, h] = (LN_EPS * np.exp(-2.0 * log_g[h] * (idx + 1.0))).astype(f32)
    for pr in range(4):
        for half in range(2):
            misc[half * 64:(half + 1) * 64, 8 + pr] = f32(np.exp(log_g[2 * pr + half] * CH))
    misc[:, 12] = LN_EPS
    ident = np.eye(128, dtype=f32)
    m = np.arange(128)
    sw = np.where(m % 64 < 32, m + 32, m - 32)
    perm = np.zeros((128, 128), f32)
    perm[sw, m] = 1.0
    return cos_t, sin_t, maskT, zeta, misc, ident, perm


def build_program():
    nc = bass.Bass("TRN2", target_bir_lowering=False)
    NDT = NDT_C
    NPAIR = NPAIR_C
    assert NDT + 2 * NPAIR == KCONV

    def din(name, shape):
        return nc.dram_tensor(name, list(shape), F32, kind="ExternalInput").ap()

    xT_d = din("xT", [D, S])
    x_d = din("x", [S, D])
    win_d = din("w_in_p", [D, NBLK * 512])
    wpw_d = din("w_pw2", [D, D])
    wout_d = din("w_out", [2 * D, D])
    vecfm_d = din("vecfm", [128, 280])
    vecpair_d = din("vecpair", [128, 8 * 2 * NPAIR])
    bcv_d = din("bcv", [128, 3 * D])
    cos_d = din("cos_t", [128, S])
    sin_d = din("sin_t", [128, S])
    maskT_d = din("maskT", [128, H * CH])
    zeta_d = din("zeta", [128, H * DK])
    misc_d = din("misc", [128, 24])
    ident_d = din("ident", [128, 128])
    perm_d = din("perm", [128, 128])
    out_d = nc.dram_tensor("out", [S, D], F32, kind="ExternalOutput").ap()
    dscr = nc.dram_tensor("diag_scr", [8, 128, 2 * NPAIR * 64], BF16, kind="Internal").ap()

    sb = SB(nc)
    p = Prog(nc)

    xT_bf = sb.alloc([128, 8, S], BF16)
    mixTc = sb.alloc([128, 8, S], BF16)
    ident_bf = sb.alloc([128, 128], BF16)
    ones_bf = sb.alloc([128, 128], BF16)
    misc = sb.alloc([128, 24], F32)
    perm_sb = sb.alloc([128, 128], F32)

    pb = [nc.alloc_psum_tensor("pb%d" % i, [128, 512], F32) for i in range(8)]
    ptr = pb[7].bitcast(BF16)

    def load_xT(tb):
        p.dma("pool", lambda e: e.dma_start(
            out=xT_bf[:, :, tb * 512:(tb + 1) * 512],
            in_=xT_d[:, tb * 512:(tb + 1) * 512].rearrange("(kc p) t -> p kc t", p=128)),
            writes=[("xT", tb)])

    p.dma("pool", lambda e: e.dma_start(out=ident_bf[:, :], in_=ident_d[:, :]), writes=["ident"])
    load_xT(0)
    p.dma("sp", lambda e: e.dma_start(out=misc[:, :], in_=misc_d[:, :]), writes=["misc"])
    p.dma("sp", lambda e: e.dma_start(out=perm_sb[:, :], in_=perm_d[:, :]), writes=["perm"])
    p.op("dve", lambda e: e.memset(ones_bf[:, :], 1.0 / 1024.0), writes=["ones"])

    def load_wblock(slot_t, slot_key, blk, deps=()):
        return p.dma("pool", lambda e: e.dma_start(
            out=slot_t[:, :, :],
            in_=win_d[:, blk * 512:(blk + 1) * 512].rearrange("(kc p) j -> p kc j", p=128)),
            writes=[slot_key], deps=deps)

    mC = sb.mark()
    conv = sb.alloc([128, 8, 1024], F32)
    diag = [sb.alloc([128, 2, NPAIR, 64], BF16) for _ in range(3)]
    ubuf = [sb.alloc([128, 1056], BF16) for _ in range(3)]
    stk = [sb.alloc([128, 2, 544], BF16) for _ in range(3)]
    vecpair = sb.alloc([128, 8 * 2 * NPAIR], F32)
    I2 = sb.alloc([128, 64], BF16)
    sig = [sb.alloc([128, 512], F32) for _ in range(2)]
    acc = [sb.alloc([128, 512], F32) for _ in range(2)]
    cb = [sb.alloc([128, 1024], BF16) for _ in range(2)]
    sq = [sb.alloc([128, 1024], BF16) for _ in range(2)]
    mean_sb = sb.alloc([128, 1024], F32)
    rstd_sb = sb.alloc([128, 1024], F32)
    tmp_sb = sb.alloc([128, 1024], F32)
    halo = sb.alloc([128, 8, 32], BF16)
    wtap_bf = sb.alloc([128, 248], BF16)
    c_dead_end = sb.mark()
    vecfm = sb.alloc([128, 280], F32)
    wslot = [sb.alloc([128, 8, 512], BF16) for _ in range(4)]
    hT = sb.alloc([128, 8, 1024], BF16)
    NSG = 3
    sg = [sb.alloc([128, 512], F32) for _ in range(NSG)]
    c_end = sb.mark()

    sb.release(mC)
    cos_sb = sb.alloc([128, S], F32)
    off_cos = sb.last_off
    sin_sb = sb.alloc([128, S], F32)
    rslot = []
    for _ in range(4):
        rslot.append(sb.alloc([128, 8, 512], BF16))
        if len(rslot) == 1:
            off_rs0 = sb.last_off
        if len(rslot) == 3:
            off_rs2 = sb.last_off
    maskT = sb.alloc([128, H * CH], F32)
    zeta = sb.alloc([128, H * DK], F32)
    gbc = sb.alloc([128, D], F32)
    assert sb.mark() <= c_dead_end, (sb.mark(), c_dead_end)
    mixTr = sb.alloc([128, 8, S], BF16)
    off_mixTr = sb.last_off
    qz = sb.alloc([128, 2, 2, S], BF16)
    kp = sb.alloc([128, 2, S], BF16)
    ktok = [sb.alloc([128, 256], BF16) for _ in range(3)]
    fbuf = [sb.alloc([128, 512], F32) for _ in range(4)]
    v_sb = [sb.alloc([128, 512], BF16) for _ in range(2)]
    ST_sb = [sb.alloc([128, 512], BF16) for _ in range(2)]
    yn = [sb.alloc([128, 512], F32) for _ in range(2)]
    ret_bf = [sb.alloc([128, 512], BF16) for _ in range(4)]
    state = sb.alloc([128, 256], F32)
    state_bf = [sb.alloc([128, 256], BF16) for _ in range(2)]
    st6 = [sb.alloc([128, 4, 6], F32) for _ in range(2)]
    mv = [sb.alloc([128, 4, 2], F32) for _ in range(2)]
    sm = [sb.alloc([128, 16], F32) for _ in range(2)]
    q32 = [sb.alloc([128, 512], F32) for _ in range(2)]
    r_end = sb.mark()
    wo = [sb.alloc_at(off_cos, [128, 8, D], BF16), sb.alloc_at(off_rs0, [128, 8, D], BF16)]

    def r_prefetch_ab(g, deps=()):
        load_wblock(rslot[0], ("rs", 0), BLK_A(g), deps)
        load_wblock(rslot[1], ("rs", 1), BLK_B(g), deps)

    def r_prefetch_vg(g, deps=()):
        load_wblock(rslot[2], ("rs", 2), BLK_V(g), deps)
        load_wblock(rslot[3], ("rs", 3), BLK_G(g), deps)


    p.dma("sp", lambda e: e.dma_start(out=vecfm[:, :], in_=vecfm_d[:, :]), writes=["vecfm"])
    VB, VG, VLB, VPB = 248, 256, 264, 272

    p.op("dve", lambda e: e.tensor_scalar(out=vecfm[:, 0:248], in0=vecfm[:, 0:248], scalar1=0.5, scalar2=None, op0=ALU.mult),
         writes=["vecfm"])
    p.dma("sp", lambda e: e.dma_start(out=vecpair[:, :], in_=vecpair_d[:, :]), writes=["vecpair"])
    p.op("dve", lambda e: e.tensor_scalar(out=vecpair[:, :], in0=vecpair[:, :], scalar1=0.5, scalar2=None, op0=ALU.mult),
         writes=["vecpair"])
    p.op("dve", lambda e: e.tensor_tensor(out=I2[:, :], in0=ident_bf[:, 0:64], in1=ident_bf[:, 64:128], op=ALU.add),
         reads=["ident"], writes=["I2"])
    NPT = KCONV - NDT
    cnt = {"item": 0}

    def s1_front(hp, tb, c):
        gb = hp * 2 + tb
        i = cnt["item"]
        cnt["item"] += 1
        ub = ubuf[i % 3]
        di = i % 3
        if gb == 0:
            p.op("pool", lambda e: e.memset(ub[:, 0:30], 0.0), writes=[("uh", i % 3)])
        else:
            p.op("pool", lambda e: e.tensor_copy(out=ub[:, 0:30], in_=halo[:, c, 0:30]),
                 reads=[("halo", c)], writes=[("uh", i % 3)])
        if gb == 0:
            def fd0(e):
                ins = None
                for ab in range(2):
                    for j in range(NPAIR):
                        col = (c * 2 + ab) * NPAIR + j
                        ins = e.tensor_scalar(out=diag[di][:, ab, j, :], in0=I2[:, :],
                                              scalar1=vecpair[:, col:col + 1], scalar2=None, op0=ALU.mult)
                return ins
            p.op("dve", fd0, reads=["I2", "vecpair"], writes=[("diag", di)])
            p.dma("sp", lambda e: e.dma_start(out=dscr[c], in_=diag[di][:, :, :, :].rearrange("p a j m -> p (a j m)")),
                  reads=[("diag", di)], writes=[("dscr", c)])
        else:
            p.dma("sp", lambda e: e.dma_start(out=diag[di][:, :, :, :].rearrange("p a j m -> p (a j m)"), in_=dscr[c]),
                  reads=[("dscr", c)], writes=[("diag", di)])
        ws = wslot[c // 2]
        co = (c % 2) * 256
        par = i % 2
        tsl = slice(gb * 512, gb * 512 + 512)

        abk = 0 if par == 0 else 4
        gbk = 1 if par == 0 else 6

        def fa(e):
            ins = None
            for kc in range(8):
                ins = e.matmul(pb[abk][:, :], lhsT=ws[:, kc, co:co + 128], rhs=xT_bf[:, kc, tsl], start=(kc == 0), stop=(kc == 7))
            return ins

        def fg(e):
            ins = None
            for kc in range(8):
                ins = e.matmul(pb[gbk][:, :], lhsT=ws[:, kc, co + 128:co + 256], rhs=xT_bf[:, kc, tsl], start=(kc == 0), stop=(kc == 7))
            return ins
        p.op("pe", fa, reads=[("ws", c // 2), ("xT", gb)], writes=[("pb", abk)])
        p.op("pe", fg, reads=[("ws", c // 2), ("xT", gb)], writes=[("pb", gbk)])
        p.op("act", lambda e: e.activation(out=sig[par][:, :], in_=pb[gbk][:, :], func=AF.Tanh, scale=0.5),
             writes=[("pb", gbk), ("sig", par)])
        p.op("dve", lambda e: e.scalar_tensor_tensor(out=ub[:, 30:542], in0=sig[par][:, :], scalar=1.0, in1=pb[abk][:, :],
                                                     op0=ALU.add, op1=ALU.mult),
             reads=[("sig", par)], writes=[("pb", abk), ("u", i % 3)])
        if gb < 3:
            p.op("pool", lambda e: e.tensor_copy(out=halo[:, c, 0:30], in_=ub[:, 512:542]),
                 reads=[("u", i % 3)], writes=[("halo", c)])
        sk = stk[i % 3]
        rk_ = [("u", i % 3), ("uh", i % 3)]
        p.op("act", lambda e: e.activation(out=sk[0:64, 0, 0:542], in_=ub[0:64, 0:542], func=AF.Copy), reads=rk_, writes=[("stk", i % 3, 0)])
        p.op("act", lambda e: e.activation(out=sk[64:128, 1, 0:542], in_=ub[64:128, 0:542], func=AF.Copy), reads=rk_, writes=[("stk", i % 3, 2)])
        p.dma("sp", lambda e: e.dma_start(out=sk[64:128, 0, 0:541], in_=ub[0:64, 1:542]), reads=rk_, writes=[("stk", i % 3, 1)])
        p.dma("act", lambda e: e.dma_start(out=sk[0:64, 1, 0:541], in_=ub[64:128, 1:542]), reads=rk_, writes=[("stk", i % 3, 3)])
        return (i, tb, c)

    def s1_conv(info):
        i, tb, c = info
        ub = ubuf[i % 3]
        par2 = i % 2
        cbk = 2 if par2 == 0 else 7
        cps = pb[cbk]
        dg = diag[i % 3]
        ac = acc[par2]
        rk = [("u", i % 3), ("uh", i % 3)]

        sk = stk[i % 3]

        def f(e):
            ins = None
            for j in range(NPAIR):
                k0 = NDT + 2 * j
                e.matmul(cps[0:64, :], lhsT=dg[:, 0, j, :], rhs=sk[:, 0, k0:k0 + 512], start=(j == 0), stop=(j == NPAIR - 1))
                ins = e.matmul(cps[64:128, :], lhsT=dg[:, 1, j, :], rhs=sk[:, 1, k0:k0 + 512], start=(j == 0), stop=(j == NPAIR - 1))
            return ins
        p.op("pe", f, reads=[("stk", i % 3, q) for q in range(4)] + [("diag", i % 3)], writes=[("pb", cbk)])
        for k in range(NDT):
            if k == 0:
                p.op("dve", lambda e, k=k: e.tensor_scalar(
                    out=ac[:, :], in0=ub[:, k:k + 512], scalar1=vecfm[:, k * 8 + c:k * 8 + c + 1], scalar2=None, op0=ALU.mult),
                    reads=rk + ["vecfm"], writes=[("acc", par2)])
            else:
                p.op("dve", lambda e, k=k: e.scalar_tensor_tensor(
                    out=ac[:, :], in0=ub[:, k:k + 512], scalar=vecfm[:, k * 8 + c:k * 8 + c + 1], in1=ac[:, :],
                    op0=ALU.mult, op1=ALU.add),
                    reads=rk + ["vecfm"], writes=[("acc", par2)])
        p.op("dve", lambda e: e.scalar_tensor_tensor(
            out=conv[:, c, tb * 512:(tb + 1) * 512], in0=cps[:, :], scalar=vecfm[:, VB + c:VB + c + 1], in1=ac[:, :],
            op0=ALU.add, op1=ALU.add),
            reads=["vecfm"], writes=[("pb", cbk), ("acc", par2), ("conv", c, tb)])

    def s1_stats(c, tb):
        b = c % 2
        bs = slice(tb * 512, (tb + 1) * 512)
        p.op("act", lambda e: e.activation(out=cb[b][:, 0:512], in_=conv[:, c, bs], func=AF.Copy),
             reads=[("conv", c, tb)], writes=[("cb", b)])
        p.op("act", lambda e: e.activation(out=sq[b][:, 0:512], in_=conv[:, c, bs], func=AF.Square),
             reads=[("conv", c, tb)], writes=[("sq", b)])

        def fs(e):
            e.matmul(pb[3][:, :], lhsT=ones_bf[:, :], rhs=cb[b][:, 0:512], start=(c == 0), stop=(c == 7))
            return e.matmul(pb[5][:, :], lhsT=ones_bf[:, :], rhs=sq[b][:, 0:512], start=(c == 0), stop=(c == 7))
        p.op("pe", fs, reads=[("cb", b), ("sq", b), "ones"], writes=[("pb", 3), ("pb", 5)])

    def s2(tb):
        bs = slice(tb * 512, (tb + 1) * 512)
        p.op("dve", lambda e: e.tensor_copy(out=mean_sb[:, bs], in_=pb[3][:, :]),
             writes=[("pb", 3), ("mean", tb)])
        p.op("dve", lambda e: e.tensor_tensor(out=tmp_sb[:, bs], in0=mean_sb[:, bs], in1=mean_sb[:, bs], op=ALU.mult),
             reads=[("mean", tb)], writes=[("tmp", tb)])
        p.op("dve", lambda e: e.tensor_tensor(out=tmp_sb[:, bs], in0=pb[5][:, :], in1=tmp_sb[:, bs], op=ALU.subtract),
             writes=[("pb", 5), ("tmp", tb)])
        p.op("act", lambda e: e.activation(out=tmp_sb[:, bs], in_=tmp_sb[:, bs], func=AF.Sqrt, bias=misc[:, 12:13], scale=1.0),
             reads=["misc"], writes=[("tmp", tb)])
        p.op("dve", lambda e: e.reciprocal(out=rstd_sb[:, bs], in_=tmp_sb[:, bs]),
             reads=[("tmp", tb)], writes=[("rstd", tb)])

    def s2b_tile(c, tb):
        bs = slice(tb * 512, (tb + 1) * 512)
        p.op("dve", lambda e: e.tensor_tensor(out=conv[:, c, bs], in0=conv[:, c, bs], in1=mean_sb[:, bs], op=ALU.subtract),
             reads=[("mean", tb)], writes=[("conv", c, tb)])
        p.op("dve", lambda e: e.tensor_tensor(out=conv[:, c, bs], in0=conv[:, c, bs], in1=rstd_sb[:, bs], op=ALU.mult),
             reads=[("rstd", tb)], writes=[("conv", c, tb)])
        p.op("act", lambda e: e.activation(out=hT[:, c, bs], in_=conv[:, c, bs], func=AF.Silu,
                                           scale=vecfm[:, VG + c:VG + c + 1], bias=vecfm[:, VLB + c:VLB + c + 1]),
             reads=[("conv", c, tb), "vecfm"], writes=[("hT", c, tb)])

    GSLOT = lambda et: 0 if et < 4 else 2
    PSLOT = lambda et: 1 if et < 4 else 3

    def s3_gate(hp, j):
        tb, et = j // 8, j % 8
        gslot = wslot[GSLOT(et)]
        eo = (et % 4) * 128
        par = j % 2
        tsl = slice((hp * 2 + tb) * 512, (hp * 2 + tb) * 512 + 512)
        sgb = sg[j % NSG]

        def fgc(e):
            ins = None
            for kc in range(8):
                ins = e.matmul(pb[par][:, :], lhsT=gslot[:, kc, eo:eo + 128], rhs=xT_bf[:, kc, tsl], start=(kc == 0), stop=(kc == 7))
            return ins
        p.op("pe", fgc, reads=[("ws", GSLOT(et)), ("xT", hp * 2 + tb)], writes=[("pb", par)])
        p.op("act", lambda e: e.activation(out=sgb[:, :], in_=pb[par][:, :], func=AF.Silu),
             writes=[("pb", par), ("sg", j % NSG)])

    def s3_pw(hp, j):
        tb, et = j // 8, j % 8
        wsl = wslot[PSLOT(et)]
        eo = (et % 4) * 128
        par = j % 2
        tsl = slice((hp * 2 + tb) * 512, (hp * 2 + tb) * 512 + 512)
        pbk = 4 if par == 0 else 6
        pwps = pb[pbk]
        sgb = sg[j % NSG]

        def fpw(e):
            ins = None
            for kc in range(8):
                ins = e.matmul(pwps[:, :], lhsT=wsl[:, kc, eo:eo + 128], rhs=hT[:, kc, tb * 512:(tb + 1) * 512],
                               start=(kc == 0), stop=(kc == 7))
            return ins
        p.op("pe", fpw, reads=[("ws", PSLOT(et))] + [("hT", kc, tb) for kc in range(8)], writes=[("pb", pbk)])
        p.op("dve", lambda e: e.scalar_tensor_tensor(
            out=mixTc[:, et, tsl], in0=pwps[:, :], scalar=vecfm[:, VPB + et:VPB + et + 1], in1=sgb[:, :],
            op0=ALU.add, op1=ALU.mult),
            reads=[("sg", j % NSG), "vecfm"], writes=[("pb", pbk)] + [("mix", 8 + et, tsl.start // 128 + q) for q in range(4)])

    def load_pw(slot_i, blk_i):
        p.dma("pool", lambda e: e.dma_start(
            out=wslot[slot_i][:, :, :],
            in_=wpw_d[:, blk_i * 512:(blk_i + 1) * 512].rearrange("(kc p) j -> p kc j", p=128)),
            writes=[("ws", slot_i)])

    def run_s1_block(hp, tb, extra=None):
        q = []
        for c in range(8):
            q.append(s1_front(hp, tb, c))
            if c >= 2:
                s1_conv(q[c - 2])
            if c >= 4:
                s1_stats(c - 4, tb)
            if extra is not None:
                extra(c)
        s1_conv(q[6])
        s1_conv(q[7])
        s1_stats(4, tb)
        s1_stats(5, tb)

    for hp in range(2):
        if hp == 0:
            load_wblock(wslot[0], ("ws", 0), BLK_AG(0))

            def extra0(c):
                if c == 0:
                    load_wblock(wslot[1], ("ws", 1), BLK_AG(1))
                if c == 1:
                    load_wblock(wslot[2], ("ws", 2), BLK_AG(2))
                if c == 2:
                    load_wblock(wslot[3], ("ws", 3), BLK_AG(3))
                if c == 3:
                    load_xT(1)
                if c == 5:
                    load_xT(2)
                if c == 7:
                    load_xT(3)
        else:
            extra0 = None
            load_wblock(wslot[2], ("ws", 2), BLK_AG(2))
            load_wblock(wslot[3], ("ws", 3), BLK_AG(3))

        run_s1_block(hp, 0, extra0)

        def extra1(c):
            if c == 0:
                s1_stats(6, 0)
                s1_stats(7, 0)
            if c == 1:
                s2(0)
            for t in {2: (0,), 3: (1,), 4: (2, 3), 5: (4,), 6: (5, 6), 7: (7,)}.get(c, ()):
                s2b_tile(t, 0)
            if c == 2:
                load_wblock(wslot[0], ("ws", 0), BLK_GC(0))
            if c == 4:
                load_pw(1, 0)
            if c == 6:
                load_wblock(wslot[2], ("ws", 2), BLK_GC(1))
        run_s1_block(hp, 1, extra1)
        load_pw(3, 1)

        s3_gate(hp, 0)
        s3_gate(hp, 1)
        for j in range(8):
            s3_pw(hp, j)
            s3_gate(hp, j + 2)
            if j == 0:
                s1_stats(6, 1)
                s1_stats(7, 1)
                s2(1)
            for t in {1: (0, 1), 2: (2, 3), 3: (4, 5), 4: (6, 7)}.get(j, ()):
                s2b_tile(t, 1)

        if hp == 1:
            snap = p.snapshot()
            p.dma("sp", lambda e: e.dma_start(out=cos_sb[:, :], in_=cos_d[:, :]), writes=["cos"], deps=snap)
            p.dma("sp", lambda e: e.dma_start(out=sin_sb[:, :], in_=sin_d[:, :]), writes=["sin"], deps=snap)
            r_prefetch_ab(0, snap)
            r_prefetch_vg(0, snap)
            p.dma("sp", lambda e: e.dma_start(out=maskT[:, :], in_=maskT_d[:, :]), writes=["maskT"], deps=snap)
            p.dma("sp", lambda e: e.dma_start(out=zeta[:, :], in_=zeta_d[:, :]), writes=["zeta"], deps=snap)
            p.dma("sp", lambda e: e.dma_start(out=gbc[:, :], in_=bcv_d[:, 0:D]), writes=["gbc"], deps=snap)

        for j in range(8, 16):
            s3_pw(hp, j)
            if j + 2 < 16:
                s3_gate(hp, j + 2)
            if hp == 0 and j == 9:
                load_wblock(wslot[0], ("ws", 0), BLK_AG(0))
            if hp == 0 and j == 11:
                load_wblock(wslot[1], ("ws", 1), BLK_AG(1))

    p.barrier(include_dma=False)

    _save = sb.mark()
    sb.release(off_rs2)
    pgb = sb.alloc([128, 2 * D], F32)
    x_sb = [sb.alloc([128, D], F32) for _ in range(2)]
    r_sb = [sb.alloc([128, D], F32) for _ in range(2)]
    ost6 = [sb.alloc([128, 2, 6], F32) for _ in range(2)]
    omv = [sb.alloc([128, 8], F32) for _ in range(2)]
    assert sb.mark() <= off_mixTr, (sb.mark(), off_mixTr)
    sb.release(_save)
    outs = []
    o_deps = []

    def load_x(t):
        p.dma("sp", lambda e: e.dma_start(out=x_sb[t % 2][:, :], in_=x_d[t * 128:(t + 1) * 128, :]), writes=[("x", t % 2)],
              deps=o_deps)

    def o_setup(deps):
        o_deps.extend(deps)
        p.dma("sp", lambda e: e.dma_start(out=pgb[:, :], in_=bcv_d[:, D:3 * D]), writes=["pgb"], deps=o_deps)
        load_x(0)

    def o_tile(t):
        par = t % 2
        rows = slice(t * 128, (t + 1) * 128)
        if t + 1 < 16:
            load_x(t + 1)
        for half in range(2):
            bk = 2 * par + half
            hps = pb[bk]

            def fo(e, hps=hps, half=half):
                ins = None
                for kc in range(16):
                    src = mixTr if kc < 8 else mixTc
                    ins = e.matmul(hps[:, :], lhsT=src[:, kc % 8, rows], rhs=wo[kc // 8][:, kc % 8, half * 512:(half + 1) * 512],
                                   start=(kc == 0), stop=(kc == 15))
                return ins
            p.op("pe", fo, reads=[("mix", kc, t) for kc in range(16)] + [("wout", q4) for q4 in range(4)],
                 writes=[("pb", bk)])
            p.op("dve", lambda e, hps=hps, half=half: e.scalar_tensor_tensor(
                out=r_sb[par][:, half * 512:(half + 1) * 512], in0=x_sb[par][:, half * 512:(half + 1) * 512], scalar=ALPHA,
                in1=hps[:, :], op0=ALU.mult, op1=ALU.add),
                reads=[("x", par)], writes=[("pb", bk), ("r", par, half)])
            p.op("dve", lambda e, half=half: e.bn_stats(out=ost6[par][:, half, :], in_=r_sb[par][:, half * 512:(half + 1) * 512]),
                 reads=[("r", par, half)], writes=[("ost6", par, half)])
        p.op("dve", lambda e: e.bn_aggr(out=omv[par][:, 0:2], in_=ost6[par][:, :, :].rearrange("p a b -> p (a b)")),
             reads=[("ost6", par, 0), ("ost6", par, 1)], writes=[("omv", par)])
        p.op("act", lambda e: e.activation(out=omv[par][:, 2:3], in_=omv[par][:, 1:2], func=AF.Sqrt, bias=misc[:, 12:13], scale=1.0),
             reads=[("omv", par), "misc"], writes=[("osd", par)])
        p.op("dve", lambda e: e.reciprocal(out=omv[par][:, 3:4], in_=omv[par][:, 2:3]),
             reads=[("osd", par)], writes=[("ors", par)])
        p.op("dve", lambda e: e.scalar_tensor_tensor(out=omv[par][:, 4:5], in0=omv[par][:, 0:1], scalar=-1.0,
                                                     in1=omv[par][:, 3:4], op0=ALU.mult, op1=ALU.mult),
             reads=[("omv", par), ("ors", par)], writes=[("onb", par)])
        p.op("act", lambda e: e.activation(out=r_sb[par][:, :], in_=r_sb[par][:, :], func=AF.Identity,
                                           scale=omv[par][:, 3:4], bias=omv[par][:, 4:5]),
             reads=[("r", par, 0), ("r", par, 1), ("ors", par), ("onb", par)], writes=[("r", par, 0), ("r", par, 1)])
        p.op("dve", lambda e: e.tensor_tensor(out=r_sb[par][:, :], in0=r_sb[par][:, :], in1=pgb[:, 0:D], op=ALU.mult),
             reads=[("r", par, 0), ("r", par, 1), "pgb"], writes=[("r", par, 0), ("r", par, 1)])
        p.op("dve" if t == 15 else "pool", lambda e: e.tensor_tensor(out=r_sb[par][:, :], in0=r_sb[par][:, :], in1=pgb[:, D:2 * D], op=ALU.add),
             reads=[("r", par, 0), ("r", par, 1), "pgb"], writes=[("r", par, 0), ("r", par, 1)])
        outs.append(p.dma("sp", lambda e: e.dma_start(out=out_d[rows, :], in_=r_sb[par][:, :]),
                          reads=[("r", par, 0), ("r", par, 1)]))

    p.op("pool", lambda e: e.memset(qz[64:128, :, 0, :], 0.0), writes=["qzpad0"])
    p.op("pool", lambda e: e.memset(qz[0:64, :, 1, :], 0.0), writes=["qzpad1"])

    def BK(i):
        return [("pb", i)]

    for g in range(2):

        blocks = [(which, slot_i, j, tb) for which, slot_i in (("q", 0), ("k", 1)) for j in range(2) for tb in range(4)]

        def prep_front(i):
            which, slot_i, j, tb = blocks[i]
            par = i % 2
            ws = rslot[slot_i]
            tsl = slice(tb * 512, (tb + 1) * 512)

            def fq(e):
                ins = None
                for kc in range(8):
                    ins = e.matmul(pb[par][:, :], lhsT=ws[:, kc, j * 128:(j + 1) * 128], rhs=xT_bf[:, kc, tsl],
                                   start=(kc == 0), stop=(kc == 7))
                return ins
            p.op("pe", fq, reads=[("rs", slot_i), ("xT", tb)], writes=BK(par))
            p.op("act", lambda e: e.activation(out=q32[par][:, :], in_=pb[par][:, :], func=AF.Copy),
                 writes=BK(par) + [("q32", par)])

        def prep_back(i):
            which, slot_i, j, tb = blocks[i]
            par = i % 2
            tsl = slice(tb * 512, (tb + 1) * 512)
            ta, tb_ = fbuf[par], fbuf[2 + par]
            p.op("pe", lambda e: e.matmul(pb[2 + par][:, :], lhsT=perm_sb[:, :], rhs=q32[par][:, :], start=True, stop=True),
                 reads=[("q32", par), "perm"], writes=BK(2 + par))
            p.op("dve", lambda e: e.tensor_tensor(out=ta[:, :], in0=q32[par][:, :], in1=cos_sb[:, tsl], op=ALU.mult),
                 reads=["cos", ("q32", par)], writes=[("fb", par)])
            p.op("dve", lambda e: e.tensor_tensor(out=tb_[:, :], in0=pb[2 + par][:, :], in1=sin_sb[:, tsl], op=ALU.mult),
                 reads=["sin"], writes=BK(2 + par) + [("fb", 2 + par)])
            if which == "k":
                p.op("pool", lambda e: e.tensor_tensor(out=kp[:, j, tsl], in0=ta[:, :], in1=tb_[:, :], op=ALU.add),
                     reads=[("fb", par), ("fb", 2 + par)], writes=[("kp", j, tb)])
            else:
                def fqa(e):
                    e.tensor_tensor(out=qz[0:64, j, 0, tsl], in0=ta[0:64, :], in1=tb_[0:64, :], op=ALU.add)
                    return e.tensor_tensor(out=qz[64:128, j, 1, tsl], in0=ta[64:128, :], in1=tb_[64:128, :], op=ALU.add)
                p.op("pool", fqa, reads=[("fb", par), ("fb", 2 + par)], writes=[("qz", j, tb)])

        def prep_step(k):
            if k < len(blocks):
                prep_front(k)
            if k >= 1:
                prep_back(k - 1)

        if g == 0:
            for k in range(len(blocks) + 1):
                prep_step(k)

        p.op("pool", lambda e: e.memset(state[:, :], 0.0), writes=["state"])
        def prefetch_wout(q4):
            p.dma("pool", lambda e: e.dma_start(
                out=wo[q4 // 2][:, (q4 % 2) * 4:(q4 % 2) * 4 + 4, :],
                in_=wout_d[q4 * 512:(q4 + 1) * 512, :].rearrange("(kc p) j -> p kc j", p=128)),
                writes=[("wout", q4)] + (["cos", "sin"] if q4 < 2 else [("rs", 0), ("rs", 1)]))

        def stage_g(n, g=g):
            csl = slice(n * 128, (n + 1) * 128)
            gb = fbuf[n % 4]

            def fg_(e):
                ins = None
                for kc in range(8):
                    ins = e.matmul(pb[1][:, :], lhsT=xT_bf[:, kc, csl], rhs=rslot[3][:, kc, :], start=(kc == 0), stop=(kc == 7))
                return ins
            p.op("pe", fg_, reads=[("rs", 3), ("xT", n // 4)], writes=BK(1))
            p.op("act", lambda e: e.activation(out=gb[:, :], in_=pb[1][:, :], func=AF.Silu),
                 writes=BK(1) + [("fb", n % 4)])
            p.op("pool", lambda e: e.tensor_tensor(out=gb[:, :], in0=gb[:, :], in1=gbc[:, g * 512:(g + 1) * 512], op=ALU.mult),
                 reads=["gbc"], writes=[("fb", n % 4)])

        def stage_v(n, g=g):
            par = n % 2
            csl = slice(n * 128, (n + 1) * 128)

            def fv(e):
                ins = None
                for kc in range(8):
                    ins = e.matmul(pb[0][:, :], lhsT=xT_bf[:, kc, csl], rhs=rslot[2][:, kc, :], start=(kc == 0), stop=(kc == 7))
                return ins
            p.op("pe", fv, reads=[("rs", 2), ("xT", n // 4)], writes=BK(0))
            p.op("act", lambda e: e.activation(out=v_sb[par][:, :], in_=pb[0][:, :], func=AF.Copy),
                 writes=BK(0) + [("v", par)])

        def stage_kt(n, g=g):
            kt = ktok[n % 3]

            kvb = pb[5].bitcast(BF16)

            def ftr(e):
                ins = None
                for j in range(2):
                    ins = e.transpose(kvb[:, j * 128:(j + 1) * 128], kp[:, j, n * 128:(n + 1) * 128], ident_bf[:, :])
                return ins
            p.op("pe", ftr, reads=[("kp", 0, n // 4), ("kp", 1, n // 4), "ident"], writes=BK(5))
            p.op("dve", lambda e: e.tensor_tensor(
                out=kt[:, :], in0=kvb[:, 0:256], in1=zeta[:, g * 256:(g + 1) * 256], op=ALU.mult),
                reads=["zeta"], writes=BK(5) + [("ktok", n % 3)])

        def stage_st(n, g=g):
            par = n % 2
            csl = slice(n * 128, (n + 1) * 128)
            sps = pb[2 + par]

            def fst(e):
                ins = None
                for j in range(2):
                    ins = e.matmul(sps[:, j * 256:(j + 1) * 256].rearrange("p (q i) -> p q i", q=2),
                                   lhsT=kp[:, j, csl], rhs=qz[:, j, :, csl], start=True, stop=True)
                return ins
            p.op("pe", fst, reads=[("kp", 0, n // 4), ("kp", 1, n // 4), ("qz", 0, n // 4), ("qz", 1, n // 4), "qzpad0", "qzpad1"],
                 writes=BK(2 + par))
            p.op("dve", lambda e: e.tensor_tensor(out=ST_sb[par][:, :], in0=sps[:, :], in1=maskT[:, g * 512:(g + 1) * 512], op=ALU.mult),
                 reads=["maskT"], writes=BK(2 + par) + [("ST", par)])

        def stage_y1(n, g=g):
            par = n % 2
            csl = slice(n * 128, (n + 1) * 128)
            yps = pb[4] if par == 0 else pb[6]
            ybk = 4 if par == 0 else 6

            def fy(e):
                ins = None
                for hh in range(4):
                    j, q = hh // 2, hh % 2
                    yo = yps[:, hh * 128:(hh + 1) * 128]
                    ins = e.matmul(yo, lhsT=ST_sb[par][:, hh * 128:(hh + 1) * 128],
                                   rhs=v_sb[par][:, hh * 128:(hh + 1) * 128], start=True, stop=(n == 0))
                    if n > 0:
                        ins = e.matmul(yo, lhsT=qz[:, j, q, csl], rhs=state_bf[par][:, j * 128:(j + 1) * 128],
                                       start=False, stop=True)
                return ins
            rd = [("ST", par), ("v", par), ("qz", 0, n // 4), ("qz", 1, n // 4), "qzpad0", "qzpad1"]
            if n > 0:
                rd.append(("sbf", par))
            p.op("pe", fy, reads=rd, writes=BK(ybk))

            if n < 15:
                def fkv(e):
                    ins = None
                    for j in range(2):
                        ins = e.matmul(pb[5][:, j * 256:(j + 1) * 256], lhsT=ktok[n % 3][:, j * 128:(j + 1) * 128],
                                       rhs=v_sb[par][:, j * 256:(j + 1) * 256], start=True, stop=True)
                    return ins
                p.op("pe", fkv, reads=[("ktok", n % 3), ("v", par)], writes=BK(5))

                def fsu(e):
                    ins = None
                    for j in range(2):
                        for hl in range(2):
                            r0 = hl * 64
                            ins = e.scalar_tensor_tensor(
                                out=state[r0:r0 + 64, j * 128:(j + 1) * 128], in0=state[r0:r0 + 64, j * 128:(j + 1) * 128],
                                scalar=misc[r0:r0 + 64, 8 + 2 * g + j: 9 + 2 * g + j],
                                in1=pb[5][r0:r0 + 64, j * 256 + hl * 128: j * 256 + hl * 128 + 128],
                                op0=ALU.mult, op1=ALU.add)
                    return ins
                p.op("dve", fsu, reads=["misc"], writes=BK(5) + ["state"])
                p.op("dve", lambda e: e.tensor_copy(out=state_bf[1 - par][:, :], in_=state[:, :]),
                     reads=["state"], writes=[("sbf", 1 - par)])

            def fbs(e):
                ins = None
                for hh in range(4):
                    ins = e.bn_stats(out=st6[par][:, hh, :], in_=yps[:, hh * 128:(hh + 1) * 128])
                return ins
            p.op("dve", fbs, writes=BK(ybk) + [("st6", par)])

            def fba(e):
                ins = None
                for hh in range(4):
                    ins = e.bn_aggr(out=mv[par][:, hh, :], in_=st6[par][:, hh, :])
                return ins
            p.op("dve", fba, reads=[("st6", par)], writes=[("mv", par)])
            p.op("dve", lambda e: e.tensor_tensor(out=sm[par][:, 0:4], in0=mv[par][:, :, 1], in1=misc[:, g * 4:(g + 1) * 4], op=ALU.add),
                 reads=[("mv", par), "misc"], writes=[("sm0", par)])
            p.op("dve", lambda e: e.tensor_scalar(out=sm[par][:, 12:16], in0=mv[par][:, :, 0], scalar1=-1.0, scalar2=None, op0=ALU.mult),
                 reads=[("mv", par)], writes=[("sm3", par)])

        def stage_y2(n, g=g):
            par = n % 2
            yps = pb[4] if par == 0 else pb[6]
            ybk = 4 if par == 0 else 6
            gb = fbuf[n % 4]
            rb = ret_bf[n % 4]
            p.op("pool", lambda e: e.tensor_tensor(out=sm[par][:, 4:8], in0=sm[par][:, 0:4], in1=misc[:, 16:20], op=ALU.pow),
                 reads=[("sm0", par), "misc"], writes=[("sm1", par)])
            p.op("pool", lambda e: e.tensor_tensor(out=sm[par][:, 8:12], in0=sm[par][:, 12:16], in1=sm[par][:, 4:8], op=ALU.mult),
                 reads=[("sm3", par), ("sm1", par)], writes=[("sm2", par)])

            def fn_(e):
                ins = None
                for hh in range(4):
                    ins = e.activation(out=yn[par][:, hh * 128:(hh + 1) * 128], in_=yps[:, hh * 128:(hh + 1) * 128],
                                       func=AF.Identity, scale=sm[par][:, 4 + hh:5 + hh], bias=sm[par][:, 8 + hh:9 + hh])
                return ins
            p.op("act", fn_, reads=[("sm1", par), ("sm2", par)], writes=BK(ybk) + [("yn", par)])
            p.op("pool", lambda e: e.tensor_tensor(out=rb[:, :], in0=yn[par][:, :], in1=gb[:, :], op=ALU.mult),
                 reads=[("yn", par), ("fb", n % 4)], writes=[("ret", n % 4)])

        def stage_tr(n, g=g):
            rb = ret_bf[n % 4]

            def ft(e):
                ins = None
                for hh in range(4):
                    ins = e.transpose(ptr[:, hh * 128:(hh + 1) * 128], rb[:, hh * 128:(hh + 1) * 128], ident_bf[:, :])
                return ins
            p.op("pe", ft, reads=[("ret", n % 4), "ident"], writes=["ptrb"])
            p.op("act", lambda e: e.activation(
                out=mixTr[:, 4 * g:4 * g + 4, n * 128:(n + 1) * 128],
                in_=ptr[:, 0:512].rearrange("p (h t) -> p h t", h=4), func=AF.Copy),
                writes=["ptrb"] + [("mix", 4 * g + hh, n) for hh in range(4)])

        for i in range(16 + 3):
            if g == 1 and i in (2, 5, 8, 11):
                prefetch_wout((i - 2) // 3)
            if g == 0 and i == 2:
                load_wblock(rslot[0], ("rs", 0), BLK_A(1))
            if g == 0 and i == 5:
                load_wblock(rslot[1], ("rs", 1), BLK_B(1))
            if g == 0 and i == 16:
                load_wblock(rslot[2], ("rs", 2), BLK_V(1))
            if g == 0 and i == 17:
                load_wblock(rslot[3], ("rs", 3), BLK_G(1))
            if i < 16:
                stage_g(i)
                stage_v(i)
                stage_kt(i)
                stage_st(i)
            if 0 <= i - 1 < 16:
                stage_y1(i - 1)
            if 0 <= i - 3 < 16:
                stage_tr(i - 3)
            if 0 <= i - 2 < 16:
                stage_y2(i - 2)
            if g == 1 and i >= 16:
                if i == 16:
                    o_setup(p.snapshot())
                o_tile(2 * (i - 16))
                o_tile(2 * (i - 16) + 1)
            if g == 0 and i >= 16:
                lo, hi = {16: (0, 2), 17: (2, 9), 18: (9, 17)}[i]
                for k in range(lo, hi):
                    prep_step(k)

    for t in range(6, 16):
        o_tile(t)

    p.emit(final_waits=outs)
    return nc


_PROGRAM = None
_CONSTS = None


def kernel(x, w_in, ret_norm_g, dw_kernel, dw_bias, conv_ln_g, conv_ln_b, w_pw2, b_pw2, w_out,
           post_ln_g, post_ln_b):
    global _PROGRAM, _CONSTS
    f32 = np.float32
    x = np.asarray(x, f32)
    w_in = np.asarray(w_in, f32)[0]
    if _CONSTS is None:
        _CONSTS = _const_tables()
    cos_t, sin_t, maskT, zeta, misc, ident, perm = _CONSTS

    w_in_p = np.ascontiguousarray(w_in[:, _w_in_perm()])
    w_pw2_ = np.ascontiguousarray(np.asarray(w_pw2, f32)[0])
    w_out_ = np.ascontiguousarray(np.asarray(w_out, f32)[0])
    vecfm = np.zeros((128, 280), f32)
    dwk = np.asarray(dw_kernel, f32)[0]
    vecfm[:, 0:248] = dwk.reshape(KCONV, 8, 128).transpose(2, 0, 1).reshape(128, 248)

    vecpair = np.zeros((128, 8 * 2 * NPAIR_C), f32)
    pidx = np.arange(128)
    for c in range(8):
        for j in range(NPAIR_C):
            k0 = NDT_C + 2 * j
            vecpair[:, (c * 2 + 0) * NPAIR_C + j] = dwk[k0 + (pidx >= 64), c * 128 + (pidx % 64)]
            vecpair[:, (c * 2 + 1) * NPAIR_C + j] = dwk[k0 + (pidx < 64), c * 128 + 64 + (pidx % 64)]

    def fm(v):
        return np.asarray(v, f32)[0].reshape(8, 128).T

    vecfm[:, 248:256] = fm(dw_bias)
    vecfm[:, 256:264] = fm(conv_ln_g)
    vecfm[:, 264:272] = fm(conv_ln_b)
    vecfm[:, 272:280] = fm(b_pw2)
    bcv = np.empty((128, 3 * D), f32)
    bcv[:, 0:D] = np.asarray(ret_norm_g, f32)[0][None, :]
    bcv[:, D:2 * D] = np.asarray(post_ln_g, f32)[0][None, :]
    bcv[:, 2 * D:3 * D] = np.asarray(post_ln_b, f32)[0][None, :]

    if _PROGRAM is None:
        _PROGRAM = build_program()
    nc = _PROGRAM

    in_maps = []
    for b in range(NCORES):
        in_maps.append({
            "xT": np.ascontiguousarray(x[b].T), "x": np.ascontiguousarray(x[b]),
            "w_in_p": w_in_p, "w_pw2": w_pw2_, "w_out": w_out_,
            "vecfm": vecfm, "vecpair": vecpair, "bcv": bcv, "cos_t": cos_t, "sin_t": sin_t,
            "maskT": maskT, "zeta": zeta, "misc": misc, "ident": ident, "perm": perm,
        })
    res = run_bass_kernel_spmd(nc, in_maps, core_ids=list(range(NCORES)))
    return np.stack([np.asarray(r["out"], f32) for r in res.results], axis=0)
```

```python
import contextlib
import numpy as np
import concourse.bass as bass
import concourse.mybir as mybir
from concourse.bass_utils import run_bass_kernel_spmd

F32 = mybir.dt.float32
BF16 = mybir.dt.bfloat16
AF = mybir.ActivationFunctionType
ALU = mybir.AluOpType

D = 1024
S = 2048
NCORES = 8
H = 8
DK = 64
DV = 128
CH = 128
KCONV = 31
NDT_C = 1
NPAIR_C = (KCONV - NDT_C) // 2
LN_EPS = 1e-5
ALPHA = 2.0 ** 0.25

ENGS = ("pe", "act", "dve", "pool", "sp")


class Op:
    __slots__ = ("eng", "fn", "deps", "sem", "val", "is_dma")

    def __init__(self, eng, fn, is_dma=False):
        self.eng = eng
        self.fn = fn
        self.deps = []
        self.sem = None
        self.val = None
        self.is_dma = is_dma


class Prog:
    def __init__(self, nc, n_dma_sems=10):
        self.nc = nc
        self.streams = {e: [] for e in ENGS}
        self.cnt = {e: 0 for e in ENGS}
        self.n_dma_sems = n_dma_sems
        self.dma_cnt = {}
        self.dma_rr = {}
        self.dma_last = {}
        self.last_writer = {}
        self.readers = {}
        self.barrier_deps = []
        self.all_dma = []

    def _track(self, o, reads, writes, deps):
        ds = list(deps) + list(self.barrier_deps)
        for k in reads:
            w = self.last_writer.get(k)
            if w is not None:
                ds.append(w)
        for k in writes:
            w = self.last_writer.get(k)
            if w is not None:
                ds.append(w)
            ds.extend(self.readers.get(k, ()))
        for k in reads:
            self.readers.setdefault(k, []).append(o)
        for k in writes:
            self.last_writer[k] = o
            self.readers[k] = []
        o.deps = [d for d in ds if d is not None and d is not o]

    def op(self, eng, fn, reads=(), writes=(), deps=()):
        o = Op(eng, fn)
        self.cnt[eng] += 1
        o.sem = eng
        o.val = self.cnt[eng]
        self._track(o, reads, writes, deps)
        self.streams[eng].append(o)
        return o

    def dma(self, eng, fn, reads=(), writes=(), deps=()):
        if eng not in self.dma_rr:
            self.dma_rr[eng] = 0
            self.dma_cnt[eng] = [0] * self.n_dma_sems
        i = self.dma_rr[eng]
        self.dma_rr[eng] = (i + 1) % self.n_dma_sems
        prev = self.dma_last.get((eng, i))
        o = Op(eng, fn, is_dma=True)
        self.dma_cnt[eng][i] += 16
        o.sem = ("dma", eng, i)
        o.val = self.dma_cnt[eng][i]
        self._track(o, reads, writes, list(deps) + ([prev] if prev is not None else []))
        self.dma_last[(eng, i)] = o
        self.streams[eng].append(o)
        self.all_dma.append(o)
        return o

    def snapshot(self):
        deps = []
        for e in ENGS:
            for o in reversed(self.streams[e]):
                if not o.is_dma:
                    deps.append(o)
                    break
        deps.extend(self.dma_last.values())
        return deps

    def barrier(self, include_dma=True):
        deps = []
        for e in ENGS:
            for o in reversed(self.streams[e]):
                if not o.is_dma:
                    deps.append(o)
                    break
        if include_dma:
            deps.extend(self.dma_last.values())
        self.barrier_deps = deps

    def emit(self, final_waits=()):
        nc = self.nc
        with contextlib.ExitStack() as st:
            sems = {}
            for e in ENGS:
                sems[e] = st.enter_context(nc.semaphore("s_" + e))
            for e in self.dma_rr:
                for i in range(self.n_dma_sems):
                    sems[("dma", e, i)] = st.enter_context(nc.semaphore("d_%s_%d" % (e, i)))
            block = st.enter_context(nc.Block())
            engobj = {"pe": block.tensor, "act": block.scalar, "dve": block.vector,
                      "pool": block.gpsimd, "sp": block.sync}

            def make(ename):
                def body(eng):
                    waited = {}

                    def wait_all(deps):
                        need = {}
                        for d in deps:
                            if need.get(d.sem, 0) < d.val:
                                need[d.sem] = d.val
                        for s, v in need.items():
                            if waited.get(s, 0) < v:
                                eng.wait_ge(sems[s], v)
                                waited[s] = v

                    for o in self.streams[ename]:
                        wait_all(o.deps)
                        ins = o.fn(eng)
                        ins.then_inc(sems[o.sem], 16 if o.is_dma else 1)
                    if ename == "sp":
                        wait_all(final_waits)
                return body

            for e in ENGS:
                if self.streams[e] or e == "sp":
                    engobj[e](make(e))


class SB:
    def __init__(self, nc):
        self.nc = nc
        self.off = (nc._sbuf_addr_for_side("left") + 63) // 64 * 64
        self.limit = nc._sbuf_addr_for_side("right")
        self.n = 0
        self.peak = 0

    def alloc(self, shape, dtype):
        esz = 4 if dtype == F32 else 2
        nbytes = int(np.prod(shape[1:])) * esz
        off = (self.off + 63) // 64 * 64
        self.n += 1
        t = self.nc.alloc_sbuf_tensor_at("sb%d" % self.n, list(shape), dtype, offset=off)
        self.off = off + nbytes
        self.peak = max(self.peak, self.off)
        self.last_off = off
        assert self.off <= self.limit, ("SBUF overflow", self.off, self.limit)
        return t

    def alloc_at(self, off, shape, dtype):
        self.n += 1
        return self.nc.alloc_sbuf_tensor_at("sb%d" % self.n, list(shape), dtype, offset=off)

    def mark(self):
        return self.off

    def release(self, m):
        self.off = m


def _w_in_perm():
    oq, ok, ov, og, oa, ogl, ogc = 0, 512, 1024, 2048, 3072, 4096, 5120
    cols = []

    def swapped(base, h):
        return list(range(base + h * 64 + 32, base + h * 64 + 64)) + list(range(base + h * 64, base + h * 64 + 32))

    for g in range(2):
        heads = range(4 * g, 4 * g + 4)
        for h in heads:
            cols += list(range(oq + h * 64, oq + h * 64 + 64))
        for h in heads:
            cols += swapped(oq, h)
        for h in heads:
            cols += list(range(ok + h * 64, ok + h * 64 + 64))
        for h in heads:
            cols += swapped(ok, h)
        cols += list(range(ov + g * 512, ov + g * 512 + 512))
        cols += list(range(og + g * 512, og + g * 512 + 512))
    for cp in range(4):
        for c in (2 * cp, 2 * cp + 1):
            cols += list(range(oa + c * 128, oa + c * 128 + 128))
            cols += list(range(ogl + c * 128, ogl + c * 128 + 128))
    cols += list(range(ogc, ogc + 1024))
    return np.asarray(cols, dtype=np.int64)


BLK_A = lambda g: 4 * g + 0
BLK_B = lambda g: 4 * g + 1
BLK_V = lambda g: 4 * g + 2
BLK_G = lambda g: 4 * g + 3
BLK_AG = lambda cp: 8 + cp
BLK_GC = lambda i: 12 + i
NBLK = 14


def _const_tables():
    f32 = np.float32
    inv_freq = 10000.0 ** (-(np.arange(32, dtype=np.float64) * 2.0) / 64.0)
    pos = np.arange(S, dtype=np.float64)
    ang = pos[:, None] * inv_freq[None, :]
    cos = np.cos(ang).astype(f32)
    sin = np.sin(ang).astype(f32)
    p = np.arange(128)
    d = p % 64
    fi = d % 32
    cos_t = np.ascontiguousarray(cos[:, fi].T)
    sgn = np.where(d < 32, -1.0, 1.0).astype(f32)
    sin_t = np.ascontiguousarray((sin[:, fi] * sgn[None, :]).T)
    hh = np.arange(H, dtype=np.float64)
    log_g = np.log1p(-np.exp2(-5.0 - hh))
    idx = np.arange(CH, dtype=np.float64)
    scale = DK ** -0.5
    causal = (idx[:, None] <= idx[None, :]).astype(np.float64)
    maskT = np.zeros((CH, H, CH), np.float64)
    for h in range(H):
        maskT[:, h, :] = scale * np.exp(-log_g[h] * (idx[:, None] + 1.0)) * causal
    maskT = maskT.reshape(CH, H * CH).astype(f32)
    zeta = np.zeros((CH, H, DK), np.float64)
    for h in range(H):
        zeta[:, h, :] = (scale * np.exp(log_g[h] * (CH - 1.0 - idx)))[:, None]
    zeta = zeta.reshape(CH, H * DK).astype(f32)
    misc = np.zeros((128, 24), f32)
    misc[:, 16:20] = -0.5
    for h in range(H):
        misc[:, h] = (LN_EPS * np.exp(-2.0 * log_g[h] * (idx + 1.0))).astype(f32)
    for pr in range(4):
        for half in range(2):
            misc[half * 64:(half + 1) * 64, 8 + pr] = f32(np.exp(log_g[2 * pr + half] * CH))
    misc[:, 12] = LN_EPS
    ident = np.eye(128, dtype=f32)
    m = np.arange(128)
    sw = np.where(m % 64 < 32, m + 32, m - 32)
    perm = np.zeros((128, 128), f32)
    perm[sw, m] = 1.0
    return cos_t, sin_t, maskT, zeta, misc, ident, perm


def build_program():
    nc = bass.Bass("TRN2", target_bir_lowering=False)
    NDT = NDT_C
    NPAIR = NPAIR_C
    assert NDT + 2 * NPAIR == KCONV

    def din(name, shape):
        return nc.dram_tensor(name, list(shape), F32, kind="ExternalInput").ap()

    xT_d = din("xT", [D, S])
    x_d = din("x", [S, D])
    win_d = din("w_in_p", [D, NBLK * 512])
    wpw_d = din("w_pw2", [D, D])
    wout_d = din("w_out", [2 * D, D])
    vecfm_d = din("vecfm", [128, 280])
    vecpair_d = din("vecpair", [128, 8 * 2 * NPAIR])
    bcv_d = din("bcv", [128, 3 * D])
    cos_d = din("cos_t", [128, S])
    sin_d = din("sin_t", [128, S])
    maskT_d = din("maskT", [128, H * CH])
    zeta_d = din("zeta", [128, H * DK])
    misc_d = din("misc", [128, 24])
    ident_d = din("ident", [128, 128])
    perm_d = din("perm", [128, 128])
    out_d = nc.dram_tensor("out", [S, D], F32, kind="ExternalOutput").ap()
    dscr = nc.dram_tensor("diag_scr", [8, 128, 2 * NPAIR * 64], BF16, kind="Internal").ap()

    sb = SB(nc)
    p = Prog(nc)

    xT_bf = sb.alloc([128, 8, S], BF16)
    mixTc = sb.alloc([128, 8, S], BF16)
    ident_bf = sb.alloc([128, 128], BF16)
    ones_bf = sb.alloc([128, 128], BF16)
    misc = sb.alloc([128, 24], F32)
    perm_sb = sb.alloc([128, 128], F32)

    pb = [nc.alloc_psum_tensor("pb%d" % i, [128, 512], F32) for i in range(8)]
    ptr = pb[7].bitcast(BF16)

    def load_xT(tb):
        p.dma("pool", lambda e: e.dma_start(
            out=xT_bf[:, :, tb * 512:(tb + 1) * 512],
            in_=xT_d[:, tb * 512:(tb + 1) * 512].rearrange("(kc p) t -> p kc t", p=128)),
            writes=[("xT", tb)])

    p.dma("pool", lambda e: e.dma_start(out=ident_bf[:, :], in_=ident_d[:, :]), writes=["ident"])
    load_xT(0)
    p.dma("sp", lambda e: e.dma_start(out=misc[:, :], in_=misc_d[:, :]), writes=["misc"])
    p.dma("sp", lambda e: e.dma_start(out=perm_sb[:, :], in_=perm_d[:, :]), writes=["perm"])
    p.op("dve", lambda e: e.memset(ones_bf[:, :], 1.0 / 1024.0), writes=["ones"])

    def load_wblock(slot_t, slot_key, blk, deps=()):
        return p.dma("pool", lambda e: e.dma_start(
            out=slot_t[:, :, :],
            in_=win_d[:, blk * 512:(blk + 1) * 512].rearrange("(kc p) j -> p kc j", p=128)),
            writes=[slot_key], deps=deps)

    mC = sb.mark()
    conv = sb.alloc([128, 8, 1024], F32)
    diag = [sb.alloc([128, 2, NPAIR, 64], BF16) for _ in range(3)]
    ubuf = [sb.alloc([128, 1056], BF16) for _ in range(3)]
    stk = [sb.alloc([128, 2, 544], BF16) for _ in range(3)]
    vecpair = sb.alloc([128, 8 * 2 * NPAIR], F32)
    I2 = sb.alloc([128, 64], BF16)
    sig = [sb.alloc([128, 512], F32) for _ in range(2)]
    acc = [sb.alloc([128, 512], F32) for _ in range(2)]
    cb = [sb.alloc([128, 1024], BF16) for _ in range(2)]
    sq = [sb.alloc([128, 1024], BF16) for _ in range(2)]
    mean_sb = sb.alloc([128, 1024], F32)
    rstd_sb = sb.alloc([128, 1024], F32)
    tmp_sb = sb.alloc([128, 1024], F32)
    halo = sb.alloc([128, 8, 32], BF16)
    wtap_bf = sb.alloc([128, 248], BF16)
    c_dead_end = sb.mark()
    vecfm = sb.alloc([128, 280], F32)
    wslot = [sb.alloc([128, 8, 512], BF16) for _ in range(4)]
    hT = sb.alloc([128, 8, 1024], BF16)
    NSG = 3
    sg = [sb.alloc([128, 512], F32) for _ in range(NSG)]
    c_end = sb.mark()

    sb.release(mC)
    cos_sb = sb.alloc([128, S], F32)
    off_cos = sb.last_off
    sin_sb = sb.alloc([128, S], F32)
    rslot = []
    for _ in range(4):
        rslot.append(sb.alloc([128, 8, 512], BF16))
        if len(rslot) == 1:
            off_rs0 = sb.last_off
        if len(rslot) == 3:
            off_rs2 = sb.last_off
    maskT = sb.alloc([128, H * CH], F32)
    zeta = sb.alloc([128, H * DK], F32)
    gbc = sb.alloc([128, D], F32)
    assert sb.mark() <= c_dead_end, (sb.mark(), c_dead_end)
    mixTr = sb.alloc([128, 8, S], BF16)
    off_mixTr = sb.last_off
    qz = sb.alloc([128, 2, 2, S], BF16)
    kp = sb.alloc([128, 2, S], BF16)
    ktok = [sb.alloc([128, 256], BF16) for _ in range(3)]
    fbuf = [sb.alloc([128, 512], F32) for _ in range(4)]
    v_sb = [sb.alloc([128, 512], BF16) for _ in range(2)]
    ST_sb = [sb.alloc([128, 512], BF16) for _ in range(2)]
    yn = [sb.alloc([128, 512], F32) for _ in range(2)]
    ret_bf = [sb.alloc([128, 512], BF16) for _ in range(4)]
    state = sb.alloc([128, 256], F32)
    state_bf = [sb.alloc([128, 256], BF16) for _ in range(2)]
    st6 = [sb.alloc([128, 4, 6], F32) for _ in range(2)]
    mv = [sb.alloc([128, 4, 2], F32) for _ in range(2)]
    sm = [sb.alloc([128, 16], F32) for _ in range(2)]
    q32 = [sb.alloc([128, 512], F32) for _ in range(2)]
    r_end = sb.mark()
    wo = [sb.alloc_at(off_cos, [128, 8, D], BF16), sb.alloc_at(off_rs0, [128, 8, D], BF16)]

    def r_prefetch_ab(g, deps=()):
        load_wblock(rslot[0], ("rs", 0), BLK_A(g), deps)
        load_wblock(rslot[1], ("rs", 1), BLK_B(g), deps)

    def r_prefetch_vg(g, deps=()):
        load_wblock(rslot[2], ("rs", 2), BLK_V(g), deps)
        load_wblock(rslot[3], ("rs", 3), BLK_G(g), deps)


    p.dma("sp", lambda e: e.dma_start(out=vecfm[:, :], in_=vecfm_d[:, :]), writes=["vecfm"])
    VB, VG, VLB, VPB = 248, 256, 264, 272

    p.op("dve", lambda e: e.tensor_scalar(out=vecfm[:, 0:248], in0=vecfm[:, 0:248], scalar1=0.5, scalar2=None, op0=ALU.mult),
         writes=["vecfm"])
    p.dma("sp", lambda e: e.dma_start(out=vecpair[:, :], in_=vecpair_d[:, :]), writes=["vecpair"])
    p.op("dve", lambda e: e.tensor_scalar(out=vecpair[:, :], in0=vecpair[:, :], scalar1=0.5, scalar2=None, op0=ALU.mult),
         writes=["vecpair"])
    p.op("dve", lambda e: e.tensor_tensor(out=I2[:, :], in0=ident_bf[:, 0:64], in1=ident_bf[:, 64:128], op=ALU.add),
         reads=["ident"], writes=["I2"])
    NPT = KCONV - NDT
    cnt = {"item": 0}

    def s1_front(hp, tb, c):
        gb = hp * 2 + tb
        i = cnt["item"]
        cnt["item"] += 1
        ub = ubuf[i % 3]
        di = i % 3
        if gb == 0:
            p.op("pool", lambda e: e.memset(ub[:, 0:30], 0.0), writes=[("uh", i % 3)])
        else:
            p.op("pool", lambda e: e.tensor_copy(out=ub[:, 0:30], in_=halo[:, c, 0:30]),
                 reads=[("halo", c)], writes=[("uh", i % 3)])
        if gb == 0:
            def fd0(e):
                ins = None
                for ab in range(2):
                    for j in range(NPAIR):
                        col = (c * 2 + ab) * NPAIR + j
                        ins = e.tensor_scalar(out=diag[di][:, ab, j, :], in0=I2[:, :],
                                              scalar1=vecpair[:, col:col + 1], scalar2=None, op0=ALU.mult)
                return ins
            p.op("dve", fd0, reads=["I2", "vecpair"], writes=[("diag", di)])
            p.dma("sp", lambda e: e.dma_start(out=dscr[c], in_=diag[di][:, :, :, :].rearrange("p a j m -> p (a j m)")),
                  reads=[("diag", di)], writes=[("dscr", c)])
        else:
            p.dma("sp", lambda e: e.dma_start(out=diag[di][:, :, :, :].rearrange("p a j m -> p (a j m)"), in_=dscr[c]),
                  reads=[("dscr", c)], writes=[("diag", di)])
        ws = wslot[c // 2]
        co = (c % 2) * 256
        par = i % 2
        tsl = slice(gb * 512, gb * 512 + 512)

        abk = 0 if par == 0 else 4
        gbk = 1 if par == 0 else 6

        def fa(e):
            ins = None
            for kc in range(8):
                ins = e.matmul(pb[abk][:, :], lhsT=ws[:, kc, co:co + 128], rhs=xT_bf[:, kc, tsl], start=(kc == 0), stop=(kc == 7))
            return ins

        def fg(e):
            ins = None
            for kc in range(8):
                ins = e.matmul(pb[gbk][:, :], lhsT=ws[:, kc, co + 128:co + 256], rhs=xT_bf[:, kc, tsl], start=(kc == 0), stop=(kc == 7))
            return ins
        p.op("pe", fa, reads=[("ws", c // 2), ("xT", gb)], writes=[("pb", abk)])
        p.op("pe", fg, reads=[("ws", c // 2), ("xT", gb)], writes=[("pb", gbk)])
        p.op("act", lambda e: e.activation(out=sig[par][:, :], in_=pb[gbk][:, :], func=AF.Tanh, scale=0.5),
             writes=[("pb", gbk), ("sig", par)])
        p.op("dve", lambda e: e.scalar_tensor_tensor(out=ub[:, 30:542], in0=sig[par][:, :], scalar=1.0, in1=pb[abk][:, :],
                                                     op0=ALU.add, op1=ALU.mult),
             reads=[("sig", par)], writes=[("pb", abk), ("u", i % 3)])
        if gb < 3:
            p.op("pool", lambda e: e.tensor_copy(out=halo[:, c, 0:30], in_=ub[:, 512:542]),
                 reads=[("u", i % 3)], writes=[("halo", c)])
        sk = stk[i % 3]
        rk_ = [("u", i % 3), ("uh", i % 3)]
        p.op("act", lambda e: e.activation(out=sk[0:64, 0, 0:542], in_=ub[0:64, 0:542], func=AF.Copy), reads=rk_, writes=[("stk", i % 3, 0)])
        p.op("act", lambda e: e.activation(out=sk[64:128, 1, 0:542], in_=ub[64:128, 0:542], func=AF.Copy), reads=rk_, writes=[("stk", i % 3, 2)])
        p.dma("sp", lambda e: e.dma_start(out=sk[64:128, 0, 0:541], in_=ub[0:64, 1:542]), reads=rk_, writes=[("stk", i % 3, 1)])
        p.dma("act", lambda e: e.dma_start(out=sk[0:64, 1, 0:541], in_=ub[64:128, 1:542]), reads=rk_, writes=[("stk", i % 3, 3)])
        return (i, tb, c)

    def s1_conv(info):
        i, tb, c = info
        ub = ubuf[i % 3]
        par2 = i % 2
        cbk = 2 if par2 == 0 else 7
        cps = pb[cbk]
        dg = diag[i % 3]
        ac = acc[par2]
        rk = [("u", i % 3), ("uh", i % 3)]

        sk = stk[i % 3]

        def f(e):
            ins = None
            for j in range(NPAIR):
                k0 = NDT + 2 * j
                e.matmul(cps[0:64, :], lhsT=dg[:, 0, j, :], rhs=sk[:, 0, k0:k0 + 512], start=(j == 0), stop=(j == NPAIR - 1))
                ins = e.matmul(cps[64:128, :], lhsT=dg[:, 1, j, :], rhs=sk[:, 1, k0:k0 + 512], start=(j == 0), stop=(j == NPAIR - 1))
            return ins
        p.op("pe", f, reads=[("stk", i % 3, q) for q in range(4)] + [("diag", i % 3)], writes=[("pb", cbk)])
        for k in range(NDT):
            if k == 0:
                p.op("dve", lambda e, k=k: e.tensor_scalar(
                    out=ac[:, :], in0=ub[:, k:k + 512], scalar1=vecfm[:, k * 8 + c:k * 8 + c + 1], scalar2=None, op0=ALU.mult),
                    reads=rk + ["vecfm"], writes=[("acc", par2)])
            else:
                p.op("dve", lambda e, k=k: e.scalar_tensor_tensor(
                    out=ac[:, :], in0=ub[:, k:k + 512], scalar=vecfm[:, k * 8 + c:k * 8 + c + 1], in1=ac[:, :],
                    op0=ALU.mult, op1=ALU.add),
                    reads=rk + ["vecfm"], writes=[("acc", par2)])
        p.op("dve", lambda e: e.scalar_tensor_tensor(
            out=conv[:, c, tb * 512:(tb + 1) * 512], in0=cps[:, :], scalar=vecfm[:, VB + c:VB + c + 1], in1=ac[:, :],
            op0=ALU.add, op1=ALU.add),
            reads=["vecfm"], writes=[("pb", cbk), ("acc", par2), ("conv", c, tb)])

    def s1_stats(c, tb):
        b = c % 2
        bs = slice(tb * 512, (tb + 1) * 512)
        p.op("act", lambda e: e.activation(out=cb[b][:, 0:512], in_=conv[:, c, bs], func=AF.Copy),
             reads=[("conv", c, tb)], writes=[("cb", b)])
        p.op("act", lambda e: e.activation(out=sq[b][:, 0:512], in_=conv[:, c, bs], func=AF.Square),
             reads=[("conv", c, tb)], writes=[("sq", b)])

        def fs(e):
            e.matmul(pb[3][:, :], lhsT=ones_bf[:, :], rhs=cb[b][:, 0:512], start=(c == 0), stop=(c == 7))
            return e.matmul(pb[5][:, :], lhsT=ones_bf[:, :], rhs=sq[b][:, 0:512], start=(c == 0), stop=(c == 7))
        p.op("pe", fs, reads=[("cb", b), ("sq", b), "ones"], writes=[("pb", 3), ("pb", 5)])

    def s2(tb):
        bs = slice(tb * 512, (tb + 1) * 512)
        p.op("dve", lambda e: e.tensor_copy(out=mean_sb[:, bs], in_=pb[3][:, :]),
             writes=[("pb", 3), ("mean", tb)])
        p.op("dve", lambda e: e.tensor_tensor(out=tmp_sb[:, bs], in0=mean_sb[:, bs], in1=mean_sb[:, bs], op=ALU.mult),
             reads=[("mean", tb)], writes=[("tmp", tb)])
        p.op("dve", lambda e: e.tensor_tensor(out=tmp_sb[:, bs], in0=pb[5][:, :], in1=tmp_sb[:, bs], op=ALU.subtract),
             writes=[("pb", 5), ("tmp", tb)])
        p.op("act", lambda e: e.activation(out=tmp_sb[:, bs], in_=tmp_sb[:, bs], func=AF.Sqrt, bias=misc[:, 12:13], scale=1.0),
             reads=["misc"], writes=[("tmp", tb)])
        p.op("dve", lambda e: e.reciprocal(out=rstd_sb[:, bs], in_=tmp_sb[:, bs]),
             reads=[("tmp", tb)], writes=[("rstd", tb)])

    def s2b_tile(c, tb):
        bs = slice(tb * 512, (tb + 1) * 512)
        p.op("dve", lambda e: e.tensor_tensor(out=conv[:, c, bs], in0=conv[:, c, bs], in1=mean_sb[:, bs], op=ALU.subtract),
             reads=[("mean", tb)], writes=[("conv", c, tb)])
        p.op("dve", lambda e: e.tensor_tensor(out=conv[:, c, bs], in0=conv[:, c, bs], in1=rstd_sb[:, bs], op=ALU.mult),
             reads=[("rstd", tb)], writes=[("conv", c, tb)])
        p.op("act", lambda e: e.activation(out=hT[:, c, bs], in_=conv[:, c, bs], func=AF.Silu,
                                           scale=vecfm[:, VG + c:VG + c + 1], bias=vecfm[:, VLB + c:VLB + c + 1]),
             reads=[("conv", c, tb), "vecfm"], writes=[("hT", c, tb)])

    GSLOT = lambda et: 0 if et < 4 else 2
    PSLOT = lambda et: 1 if et < 4 else 3

    def s3_gate(hp, j):
        tb, et = j // 8, j % 8
        gslot = wslot[GSLOT(et)]
        eo = (et % 4) * 128
        par = j % 2
        tsl = slice((hp * 2 + tb) * 512, (hp * 2 + tb) * 512 + 512)
        sgb = sg[j % NSG]

        def fgc(e):
            ins = None
            for kc in range(8):
                ins = e.matmul(pb[par][:, :], lhsT=gslot[:, kc, eo:eo + 128], rhs=xT_bf[:, kc, tsl], start=(kc == 0), stop=(kc == 7))
            return ins
        p.op("pe", fgc, reads=[("ws", GSLOT(et)), ("xT", hp * 2 + tb)], writes=[("pb", par)])
        p.op("act", lambda e: e.activation(out=sgb[:, :], in_=pb[par][:, :], func=AF.Silu),
             writes=[("pb", par), ("sg", j % NSG)])

    def s3_pw(hp, j):
        tb, et = j // 8, j % 8
        wsl = wslot[PSLOT(et)]
        eo = (et % 4) * 128
        par = j % 2
        tsl = slice((hp * 2 + tb) * 512, (hp * 2 + tb) * 512 + 512)
        pbk = 4 if par == 0 else 6
        pwps = pb[pbk]
        sgb = sg[j % NSG]

        def fpw(e):
            ins = None
            for kc in range(8):
                ins = e.matmul(pwps[:, :], lhsT=wsl[:, kc, eo:eo + 128], rhs=hT[:, kc, tb * 512:(tb + 1) * 512],
                               start=(kc == 0), stop=(kc == 7))
            return ins
        p.op("pe", fpw, reads=[("ws", PSLOT(et))] + [("hT", kc, tb) for kc in range(8)], writes=[("pb", pbk)])
        p.op("dve", lambda e: e.scalar_tensor_tensor(
            out=mixTc[:, et, tsl], in0=pwps[:, :], scalar=vecfm[:, VPB + et:VPB + et + 1], in1=sgb[:, :],
            op0=ALU.add, op1=ALU.mult),
            reads=[("sg", j % NSG), "vecfm"], writes=[("pb", pbk)] + [("mix", 8 + et, tsl.start // 128 + q) for q in range(4)])

    def load_pw(slot_i, blk_i):
        p.dma("pool", lambda e: e.dma_start(
            out=wslot[slot_i][:, :, :],
            in_=wpw_d[:, blk_i * 512:(blk_i + 1) * 512].rearrange("(kc p) j -> p kc j", p=128)),
            writes=[("ws", slot_i)])

    def run_s1_block(hp, tb, extra=None):
        q = []
        for c in range(8):
            q.append(s1_front(hp, tb, c))
            if c >= 2:
                s1_conv(q[c - 2])
            if c >= 4:
                s1_stats(c - 4, tb)
            if extra is not None:
                extra(c)
        s1_conv(q[6])
        s1_conv(q[7])
        s1_stats(4, tb)
        s1_stats(5, tb)

    for hp in range(2):
        if hp == 0:
            load_wblock(wslot[0], ("ws", 0), BLK_AG(0))

            def extra0(c):
                if c == 0:
                    load_wblock(wslot[1], ("ws", 1), BLK_AG(1))
                if c == 1:
                    load_wblock(wslot[2], ("ws", 2), BLK_AG(2))
                if c == 2:
                    load_wblock(wslot[3], ("ws", 3), BLK_AG(3))
                if c == 3:
                    load_xT(1)
                if c == 5:
                    load_xT(2)
                if c == 7:
                    load_xT(3)
        else:
            extra0 = None
            load_wblock(wslot[2], ("ws", 2), BLK_AG(2))
            load_wblock(wslot[3], ("ws", 3), BLK_AG(3))

        run_s1_block(hp, 0, extra0)

        def extra1(c):
            if c == 0:
                s1_stats(6, 0)
                s1_stats(7, 0)
            if c == 1:
                s2(0)
            for t in {2: (0,), 3: (1,), 4: (2, 3), 5: (4,), 6: (5, 6), 7: (7,)}.get(c, ()):
                s2b_tile(t, 0)
            if c == 2:
                load_wblock(wslot[0], ("ws", 0), BLK_GC(0))
            if c == 4:
                load_pw(1, 0)
            if c == 6:
                load_wblock(wslot[2], ("ws", 2), BLK_GC(1))
        run_s1_block(hp, 1, extra1)
        load_pw(3, 1)

        s3_gate(hp, 0)
        s3_gate(hp, 1)
        for j in range(8):
            s3_pw(hp, j)
            s3_gate(hp, j + 2)
            if j == 0:
                s1_stats(6, 1)
                s1_stats(7, 1)
                s2(1)
            for t in {1: (0, 1), 2: (2, 3), 3: (4, 5), 4: (6, 7)}.get(j, ()):
                s2b_tile(t, 1)

        if hp == 1:
            snap = p.snapshot()
            p.dma("sp", lambda e: e.dma_start(out=cos_sb[:, :], in_=cos_d[:, :]), writes=["cos"], deps=snap)
            p.dma("sp", lambda e: e.dma_start(out=sin_sb[:, :], in_=sin_d[:, :]), writes=["sin"], deps=snap)
            r_prefetch_ab(0, snap)
            r_prefetch_vg(0, snap)
            p.dma("sp", lambda e: e.dma_start(out=maskT[:, :], in_=maskT_d[:, :]), writes=["maskT"], deps=snap)
            p.dma("sp", lambda e: e.dma_start(out=zeta[:, :], in_=zeta_d[:, :]), writes=["zeta"], deps=snap)
            p.dma("sp", lambda e: e.dma_start(out=gbc[:, :], in_=bcv_d[:, 0:D]), writes=["gbc"], deps=snap)

        for j in range(8, 16):
            s3_pw(hp, j)
            if j + 2 < 16:
                s3_gate(hp, j + 2)
            if hp == 0 and j == 9:
                load_wblock(wslot[0], ("ws", 0), BLK_AG(0))
            if hp == 0 and j == 11:
                load_wblock(wslot[1], ("ws", 1), BLK_AG(1))

    p.barrier(include_dma=False)

    _save = sb.mark()
    sb.release(off_rs2)
    pgb = sb.alloc([128, 2 * D], F32)
    x_sb = [sb.alloc([128, D], F32) for _ in range(2)]
    r_sb = [sb.alloc([128, D], F32) for _ in range(2)]
    ost6 = [sb.alloc([128, 2, 6], F32) for _ in range(2)]
    omv = [sb.alloc([128, 8], F32) for _ in range(2)]
    assert sb.mark() <= off_mixTr, (sb.mark(), off_mixTr)
    sb.release(_save)
    outs = []
    o_deps = []

    def load_x(t):
        p.dma("act", lambda e: e.dma_start(out=x_sb[t % 2][:, :], in_=x_d[t * 128:(t + 1) * 128, :]), writes=[("x", t % 2)],
              deps=o_deps)

    def o_setup(deps):
        o_deps.extend(deps)
        p.dma("sp", lambda e: e.dma_start(out=pgb[:, :], in_=bcv_d[:, D:3 * D]), writes=["pgb"], deps=o_deps)
        load_x(0)

    def o_tile(t):
        par = t % 2
        rows = slice(t * 128, (t + 1) * 128)
        if t + 1 < 16:
            load_x(t + 1)
        for half in range(2):
            bk = 2 * par + half
            hps = pb[bk]

            def fo(e, hps=hps, half=half):
                ins = None
                for kc in range(16):
                    src = mixTr if kc < 8 else mixTc
                    ins = e.matmul(hps[:, :], lhsT=src[:, kc % 8, rows], rhs=wo[kc // 8][:, kc % 8, half * 512:(half + 1) * 512],
                                   start=(kc == 0), stop=(kc == 15))
                return ins
            p.op("pe", fo, reads=[("mix", kc, t) for kc in range(16)] + [("wout", q4) for q4 in range(4)],
                 writes=[("pb", bk)])
            p.op("dve", lambda e, hps=hps, half=half: e.scalar_tensor_tensor(
                out=r_sb[par][:, half * 512:(half + 1) * 512], in0=x_sb[par][:, half * 512:(half + 1) * 512], scalar=ALPHA,
                in1=hps[:, :], op0=ALU.mult, op1=ALU.add),
                reads=[("x", par)], writes=[("pb", bk), ("r", par, half)])
            p.op("dve", lambda e, half=half: e.bn_stats(out=ost6[par][:, half, :], in_=r_sb[par][:, half * 512:(half + 1) * 512]),
                 reads=[("r", par, half)], writes=[("ost6", par, half)])
        p.op("dve", lambda e: e.bn_aggr(out=omv[par][:, 0:2], in_=ost6[par][:, :, :].rearrange("p a b -> p (a b)")),
             reads=[("ost6", par, 0), ("ost6", par, 1)], writes=[("omv", par)])
        p.op("act", lambda e: e.activation(out=omv[par][:, 2:3], in_=omv[par][:, 1:2], func=AF.Sqrt, bias=misc[:, 12:13], scale=1.0),
             reads=[("omv", par), "misc"], writes=[("osd", par)])
        p.op("dve", lambda e: e.reciprocal(out=omv[par][:, 3:4], in_=omv[par][:, 2:3]),
             reads=[("osd", par)], writes=[("ors", par)])
        p.op("dve", lambda e: e.scalar_tensor_tensor(out=omv[par][:, 4:5], in0=omv[par][:, 0:1], scalar=-1.0,
                                                     in1=omv[par][:, 3:4], op0=ALU.mult, op1=ALU.mult),
             reads=[("omv", par), ("ors", par)], writes=[("onb", par)])
        p.op("act", lambda e: e.activation(out=r_sb[par][:, :], in_=r_sb[par][:, :], func=AF.Identity,
                                           scale=omv[par][:, 3:4], bias=omv[par][:, 4:5]),
             reads=[("r", par, 0), ("r", par, 1), ("ors", par), ("onb", par)], writes=[("r", par, 0), ("r", par, 1)])
        p.op("dve", lambda e: e.tensor_tensor(out=r_sb[par][:, :], in0=r_sb[par][:, :], in1=pgb[:, 0:D], op=ALU.mult),
             reads=[("r", par, 0), ("r", par, 1), "pgb"], writes=[("r", par, 0), ("r", par, 1)])
        p.op("pool", lambda e: e.tensor_tensor(out=r_sb[par][:, :], in0=r_sb[par][:, :], in1=pgb[:, D:2 * D], op=ALU.add),
             reads=[("r", par, 0), ("r", par, 1), "pgb"], writes=[("r", par, 0), ("r", par, 1)])
        outs.append(p.dma("sp", lambda e: e.dma_start(out=out_d[rows, :], in_=r_sb[par][:, :]),
                          reads=[("r", par, 0), ("r", par, 1)]))

    p.op("pool", lambda e: e.memset(qz[64:128, :, 0, :], 0.0), writes=["qzpad0"])
    p.op("pool", lambda e: e.memset(qz[0:64, :, 1, :], 0.0), writes=["qzpad1"])

    def BK(i):
        return [("pb", i)]

    for g in range(2):

        blocks = [(which, slot_i, j, tb) for which, slot_i in (("q", 0), ("k", 1)) for j in range(2) for tb in range(4)]

        def prep_front(i):
            which, slot_i, j, tb = blocks[i]
            par = i % 2
            ws = rslot[slot_i]
            tsl = slice(tb * 512, (tb + 1) * 512)

            def fq(e):
                ins = None
                for kc in range(8):
                    ins = e.matmul(pb[par][:, :], lhsT=ws[:, kc, j * 128:(j + 1) * 128], rhs=xT_bf[:, kc, tsl],
                                   start=(kc == 0), stop=(kc == 7))
                return ins
            p.op("pe", fq, reads=[("rs", slot_i), ("xT", tb)], writes=BK(par))
            p.op("act", lambda e: e.activation(out=q32[par][:, :], in_=pb[par][:, :], func=AF.Copy),
                 writes=BK(par) + [("q32", par)])

        def prep_back(i):
            which, slot_i, j, tb = blocks[i]
            par = i % 2
            tsl = slice(tb * 512, (tb + 1) * 512)
            ta, tb_ = fbuf[par], fbuf[2 + par]
            p.op("pe", lambda e: e.matmul(pb[2 + par][:, :], lhsT=perm_sb[:, :], rhs=q32[par][:, :], start=True, stop=True),
                 reads=[("q32", par), "perm"], writes=BK(2 + par))
            p.op("dve", lambda e: e.tensor_tensor(out=ta[:, :], in0=q32[par][:, :], in1=cos_sb[:, tsl], op=ALU.mult),
                 reads=["cos", ("q32", par)], writes=[("fb", par)])
            p.op("dve", lambda e: e.tensor_tensor(out=tb_[:, :], in0=pb[2 + par][:, :], in1=sin_sb[:, tsl], op=ALU.mult),
                 reads=["sin"], writes=BK(2 + par) + [("fb", 2 + par)])
            if which == "k":
                p.op("pool", lambda e: e.tensor_tensor(out=kp[:, j, tsl], in0=ta[:, :], in1=tb_[:, :], op=ALU.add),
                     reads=[("fb", par), ("fb", 2 + par)], writes=[("kp", j, tb)])
            else:
                def fqa(e):
                    e.tensor_tensor(out=qz[0:64, j, 0, tsl], in0=ta[0:64, :], in1=tb_[0:64, :], op=ALU.add)
                    return e.tensor_tensor(out=qz[64:128, j, 1, tsl], in0=ta[64:128, :], in1=tb_[64:128, :], op=ALU.add)
                p.op("pool", fqa, reads=[("fb", par), ("fb", 2 + par)], writes=[("qz", j, tb)])

        def prep_step(k):
            if k < len(blocks):
                prep_front(k)
            if k >= 1:
                prep_back(k - 1)

        if g == 0:
            for k in range(len(blocks) + 1):
                prep_step(k)

        p.op("pool", lambda e: e.memset(state[:, :], 0.0), writes=["state"])
        def prefetch_wout(q4):
            p.dma("pool", lambda e: e.dma_start(
                out=wo[q4 // 2][:, (q4 % 2) * 4:(q4 % 2) * 4 + 4, :],
                in_=wout_d[q4 * 512:(q4 + 1) * 512, :].rearrange("(kc p) j -> p kc j", p=128)),
                writes=[("wout", q4)] + (["cos", "sin"] if q4 < 2 else [("rs", 0), ("rs", 1)]))

        def stage_g(n, g=g):
            csl = slice(n * 128, (n + 1) * 128)
            gb = fbuf[n % 4]

            def fg_(e):
                ins = None
                for kc in range(8):
                    ins = e.matmul(pb[1][:, :], lhsT=xT_bf[:, kc, csl], rhs=rslot[3][:, kc, :], start=(kc == 0), stop=(kc == 7))
                return ins
            p.op("pe", fg_, reads=[("rs", 3), ("xT", n // 4)], writes=BK(1))
            p.op("act", lambda e: e.activation(out=gb[:, :], in_=pb[1][:, :], func=AF.Silu),
                 writes=BK(1) + [("fb", n % 4)])
            p.op("pool", lambda e: e.tensor_tensor(out=gb[:, :], in0=gb[:, :], in1=gbc[:, g * 512:(g + 1) * 512], op=ALU.mult),
                 reads=["gbc"], writes=[("fb", n % 4)])

        def stage_v(n, g=g):
            par = n % 2
            csl = slice(n * 128, (n + 1) * 128)

            def fv(e):
                ins = None
                for kc in range(8):
                    ins = e.matmul(pb[0][:, :], lhsT=xT_bf[:, kc, csl], rhs=rslot[2][:, kc, :], start=(kc == 0), stop=(kc == 7))
                return ins
            p.op("pe", fv, reads=[("rs", 2), ("xT", n // 4)], writes=BK(0))
            p.op("act", lambda e: e.activation(out=v_sb[par][:, :], in_=pb[0][:, :], func=AF.Copy),
                 writes=BK(0) + [("v", par)])

        def stage_kt(n, g=g):
            kt = ktok[n % 3]

            kvb = pb[5].bitcast(BF16)

            def ftr(e):
                ins = None
                for j in range(2):
                    ins = e.transpose(kvb[:, j * 128:(j + 1) * 128], kp[:, j, n * 128:(n + 1) * 128], ident_bf[:, :])
                return ins
            p.op("pe", ftr, reads=[("kp", 0, n // 4), ("kp", 1, n // 4), "ident"], writes=BK(5))
            p.op("dve", lambda e: e.tensor_tensor(
                out=kt[:, :], in0=kvb[:, 0:256], in1=zeta[:, g * 256:(g + 1) * 256], op=ALU.mult),
                reads=["zeta"], writes=BK(5) + [("ktok", n % 3)])

        def stage_st(n, g=g):
            par = n % 2
            csl = slice(n * 128, (n + 1) * 128)
            sps = pb[2 + par]

            def fst(e):
                ins = None
                for j in range(2):
                    ins = e.matmul(sps[:, j * 256:(j + 1) * 256].rearrange("p (q i) -> p q i", q=2),
                                   lhsT=kp[:, j, csl], rhs=qz[:, j, :, csl], start=True, stop=True)
                return ins
            p.op("pe", fst, reads=[("kp", 0, n // 4), ("kp", 1, n // 4), ("qz", 0, n // 4), ("qz", 1, n // 4), "qzpad0", "qzpad1"],
                 writes=BK(2 + par))
            p.op("dve", lambda e: e.tensor_tensor(out=ST_sb[par][:, :], in0=sps[:, :], in1=maskT[:, g * 512:(g + 1) * 512], op=ALU.mult),
                 reads=["maskT"], writes=BK(2 + par) + [("ST", par)])

        def stage_y1(n, g=g):
            par = n % 2
            csl = slice(n * 128, (n + 1) * 128)
            yps = pb[4] if par == 0 else pb[6]
            ybk = 4 if par == 0 else 6

            def fy(e):
                ins = None
                for hh in range(4):
                    j, q = hh // 2, hh % 2
                    yo = yps[:, hh * 128:(hh + 1) * 128]
                    ins = e.matmul(yo, lhsT=ST_sb[par][:, hh * 128:(hh + 1) * 128],
                                   rhs=v_sb[par][:, hh * 128:(hh + 1) * 128], start=True, stop=(n == 0))
                    if n > 0:
                        ins = e.matmul(yo, lhsT=qz[:, j, q, csl], rhs=state_bf[par][:, j * 128:(j + 1) * 128],
                                       start=False, stop=True)
                return ins
            rd = [("ST", par), ("v", par), ("qz", 0, n // 4), ("qz", 1, n // 4), "qzpad0", "qzpad1"]
            if n > 0:
                rd.append(("sbf", par))
            p.op("pe", fy, reads=rd, writes=BK(ybk))

            if n < 15:
                def fkv(e):
                    ins = None
                    for j in range(2):
                        ins = e.matmul(pb[5][:, j * 256:(j + 1) * 256], lhsT=ktok[n % 3][:, j * 128:(j + 1) * 128],
                                       rhs=v_sb[par][:, j * 256:(j + 1) * 256], start=True, stop=True)
                    return ins
                p.op("pe", fkv, reads=[("ktok", n % 3), ("v", par)], writes=BK(5))

                def fsu(e):
                    ins = None
                    for j in range(2):
                        for hl in range(2):
                            r0 = hl * 64
                            ins = e.scalar_tensor_tensor(
                                out=state[r0:r0 + 64, j * 128:(j + 1) * 128], in0=state[r0:r0 + 64, j * 128:(j + 1) * 128],
                                scalar=misc[r0:r0 + 64, 8 + 2 * g + j: 9 + 2 * g + j],
                                in1=pb[5][r0:r0 + 64, j * 256 + hl * 128: j * 256 + hl * 128 + 128],
                                op0=ALU.mult, op1=ALU.add)
                    return ins
                p.op("dve", fsu, reads=["misc"], writes=BK(5) + ["state"])
                p.op("dve", lambda e: e.tensor_copy(out=state_bf[1 - par][:, :], in_=state[:, :]),
                     reads=["state"], writes=[("sbf", 1 - par)])

            def fbs(e):
                ins = None
                for hh in range(4):
                    ins = e.bn_stats(out=st6[par][:, hh, :], in_=yps[:, hh * 128:(hh + 1) * 128])
                return ins
            p.op("dve", fbs, writes=BK(ybk) + [("st6", par)])

            def fba(e):
                ins = None
                for hh in range(4):
                    ins = e.bn_aggr(out=mv[par][:, hh, :], in_=st6[par][:, hh, :])
                return ins
            p.op("dve", fba, reads=[("st6", par)], writes=[("mv", par)])
            p.op("dve", lambda e: e.tensor_tensor(out=sm[par][:, 0:4], in0=mv[par][:, :, 1], in1=misc[:, g * 4:(g + 1) * 4], op=ALU.add),
                 reads=[("mv", par), "misc"], writes=[("sm0", par)])
            p.op("dve", lambda e: e.tensor_scalar(out=sm[par][:, 12:16], in0=mv[par][:, :, 0], scalar1=-1.0, scalar2=None, op0=ALU.mult),
                 reads=[("mv", par)], writes=[("sm3", par)])

        def stage_y2(n, g=g):
            par = n % 2
            yps = pb[4] if par == 0 else pb[6]
            ybk = 4 if par == 0 else 6
            gb = fbuf[n % 4]
            rb = ret_bf[n % 4]
            p.op("pool", lambda e: e.tensor_tensor(out=sm[par][:, 4:8], in0=sm[par][:, 0:4], in1=misc[:, 16:20], op=ALU.pow),
                 reads=[("sm0", par), "misc"], writes=[("sm1", par)])
            p.op("pool", lambda e: e.tensor_tensor(out=sm[par][:, 8:12], in0=sm[par][:, 12:16], in1=sm[par][:, 4:8], op=ALU.mult),
                 reads=[("sm3", par), ("sm1", par)], writes=[("sm2", par)])

            def fn_(e):
                ins = None
                for hh in range(4):
                    ins = e.activation(out=yn[par][:, hh * 128:(hh + 1) * 128], in_=yps[:, hh * 128:(hh + 1) * 128],
                                       func=AF.Identity, scale=sm[par][:, 4 + hh:5 + hh], bias=sm[par][:, 8 + hh:9 + hh])
                return ins
            p.op("act", fn_, reads=[("sm1", par), ("sm2", par)], writes=BK(ybk) + [("yn", par)])
            p.op("pool", lambda e: e.tensor_tensor(out=rb[:, :], in0=yn[par][:, :], in1=gb[:, :], op=ALU.mult),
                 reads=[("yn", par), ("fb", n % 4)], writes=[("ret", n % 4)])

        def stage_tr(n, g=g):
            rb = ret_bf[n % 4]

            def ft(e):
                ins = None
                for hh in range(4):
                    ins = e.transpose(ptr[:, hh * 128:(hh + 1) * 128], rb[:, hh * 128:(hh + 1) * 128], ident_bf[:, :])
                return ins
            p.op("pe", ft, reads=[("ret", n % 4), "ident"], writes=["ptrb"])
            p.op("act", lambda e: e.activation(
                out=mixTr[:, 4 * g:4 * g + 4, n * 128:(n + 1) * 128],
                in_=ptr[:, 0:512].rearrange("p (h t) -> p h t", h=4), func=AF.Copy),
                writes=["ptrb"] + [("mix", 4 * g + hh, n) for hh in range(4)])

        for i in range(16 + 3):
            if g == 1 and i in (2, 5, 8, 11):
                prefetch_wout((i - 2) // 3)
            if g == 0 and i == 2:
                load_wblock(rslot[0], ("rs", 0), BLK_A(1))
            if g == 0 and i == 5:
                load_wblock(rslot[1], ("rs", 1), BLK_B(1))
            if g == 0 and i == 16:
                load_wblock(rslot[2], ("rs", 2), BLK_V(1))
            if g == 0 and i == 17:
                load_wblock(rslot[3], ("rs", 3), BLK_G(1))
            if i < 16:
                stage_g(i)
                stage_v(i)
                stage_kt(i)
                stage_st(i)
            if 0 <= i - 1 < 16:
                stage_y1(i - 1)
            if 0 <= i - 3 < 16:
                stage_tr(i - 3)
            if 0 <= i - 2 < 16:
                stage_y2(i - 2)
            if g == 1 and i >= 16:
                if i == 16:
                    o_setup(p.snapshot())
                o_tile(2 * (i - 16))
                o_tile(2 * (i - 16) + 1)
            if g == 0 and i >= 16:
                lo, hi = {16: (0, 2), 17: (2, 9), 18: (9, 17)}[i]
                for k in range(lo, hi):
                    prep_step(k)

    for t in range(6, 16):
        o_tile(t)

    p.emit(final_waits=outs)
    return nc


_PROGRAM = None
_CONSTS = None


def kernel(x, w_in, ret_norm_g, dw_kernel, dw_bias, conv_ln_g, conv_ln_b, w_pw2, b_pw2, w_out,
           post_ln_g, post_ln_b):
    global _PROGRAM, _CONSTS
    f32 = np.float32
    x = np.asarray(x, f32)
    w_in = np.asarray(w_in, f32)[0]
    if _CONSTS is None:
        _CONSTS = _const_tables()
    cos_t, sin_t, maskT, zeta, misc, ident, perm = _CONSTS

    w_in_p = np.ascontiguousarray(w_in[:, _w_in_perm()])
    w_pw2_ = np.ascontiguousarray(np.asarray(w_pw2, f32)[0])
    w_out_ = np.ascontiguousarray(np.asarray(w_out, f32)[0])
    vecfm = np.zeros((128, 280), f32)
    dwk = np.asarray(dw_kernel, f32)[0]
    vecfm[:, 0:248] = dwk.reshape(KCONV, 8, 128).transpose(2, 0, 1).reshape(128, 248)

    vecpair = np.zeros((128, 8 * 2 * NPAIR_C), f32)
    pidx = np.arange(128)
    for c in range(8):
        for j in range(NPAIR_C):
            k0 = NDT_C + 2 * j
            vecpair[:, (c * 2 + 0) * NPAIR_C + j] = dwk[k0 + (pidx >= 64), c * 128 + (pidx % 64)]
            vecpair[:, (c * 2 + 1) * NPAIR_C + j] = dwk[k0 + (pidx < 64), c * 128 + 64 + (pidx % 64)]

    def fm(v):
        return np.asarray(v, f32)[0].reshape(8, 128).T

    vecfm[:, 248:256] = fm(dw_bias)
    vecfm[:, 256:264] = fm(conv_ln_g)
    vecfm[:, 264:272] = fm(conv_ln_b)
    vecfm[:, 272:280] = fm(b_pw2)
    bcv = np.empty((128, 3 * D), f32)
    bcv[:, 0:D] = np.asarray(ret_norm_g, f32)[0][None, :]
    bcv[:, D:2 * D] = np.asarray(post_ln_g, f32)[0][None, :]
    bcv[:, 2 * D:3 * D] = np.asarray(post_ln_b, f32)[0][None, :]

    if _PROGRAM is None:
        _PROGRAM = build_program()
    nc = _PROGRAM

    in_maps = []
    for b in range(NCORES):
        in_maps.append({
            "xT": np.ascontiguousarray(x[b].T), "x": np.ascontiguousarray(x[b]),
            "w_in_p": w_in_p, "w_pw2": w_pw2_, "w_out": w_out_,
            "vecfm": vecfm, "vecpair": vecpair, "bcv": bcv, "cos_t": cos_t, "sin_t": sin_t,
            "maskT": maskT, "zeta": zeta, "misc": misc, "ident": ident, "perm": perm,
        })
    res = run_bass_kernel_spmd(nc, in_maps, core_ids=list(range(NCORES)))
    return np.stack([np.asarray(r["out"], f32) for r in res.results], axis=0)
```
